# Optimizing a Trainium2 kernel written in Bass

```python
import math
import jax, jax.numpy as jnp
from jax import lax
import numpy as np

D_MODEL = 2048
BATCH = 4
SEQ = 2048
DEPTH = 4
DEC_BATCH = 128
DEC_SEQ = 1
PAST_LEN = 16384
PAGE_SIZE = 128

D_CONV = D_MODEL // 2
D_GDN = D_MODEL - D_CONV
GDN_HEADS = 8
HEAD_DIM = D_GDN // GDN_HEADS
CONV_A_WIDTH = 3
CONV_QKV_WIDTH = 4
CHUNK = 64
EPS = 1e-6
SPLIT_SIZES = (D_CONV, D_CONV, D_CONV, D_CONV, 3 * D_GDN, D_GDN, GDN_HEADS, GDN_HEADS)
SPLIT_POINTS = tuple(int(v) for v in np.cumsum(SPLIT_SIZES)[:-1])
D_IN_PROJ = sum(SPLIT_SIZES)

kernel_name = 'hybrid_shortconv_gdn_adaln_step'


def rmsnorm(x, g):
    xf = x.astype(jnp.float32)
    y = xf * lax.rsqrt(jnp.mean(xf * xf, axis=-1, keepdims=True) + EPS)
    return (y * g.astype(jnp.float32)).astype(x.dtype)


def l2norm(x):
    return x * lax.rsqrt(jnp.sum(x * x, axis=-1, keepdims=True) + EPS)


def causal_depthwise_conv(x, buf, w):
    width = w.shape[0]
    t = x.shape[1]
    xx = jnp.concatenate([buf.astype(x.dtype), x], axis=1)
    y = sum(xx[:, j:j + t] * w[j].astype(x.dtype) for j in range(width))
    return y, xx[:, t:]


def gated_delta_rule(q, k, v, g, beta, s0):
    b, t, h, dk = q.shape
    dv = v.shape[-1]
    f32 = jnp.float32
    c = min(CHUNK, t)
    n = -(-t // c)
    pad = n * c - t

    def prep(a):
        a = a.astype(f32)
        a = jnp.pad(a, [(0, 0), (0, pad)] + [(0, 0)] * (a.ndim - 2))
        a = a.reshape((b, n, c) + a.shape[2:])
        return jnp.moveaxis(a, 3, 1)

    q, k, v, g, beta = prep(q), prep(k), prep(v), prep(g), prep(beta)
    q = q * (dk ** -0.5)
    gc = jnp.cumsum(g, axis=-1)
    idx = jnp.arange(c)
    strict = idx[:, None] > idx[None, :]
    causal = idx[:, None] >= idx[None, :]
    diff = gc[..., :, None] - gc[..., None, :]
    decay_strict = jnp.exp(jnp.where(strict, diff, -jnp.inf))
    decay_causal = jnp.exp(jnp.where(causal, diff, -jnp.inf))
    kb = k * beta[..., None]
    lmat = jnp.einsum('bhnid,bhnjd->bhnij', kb, k) * decay_strict
    eye = jnp.eye(c, dtype=f32)
    tmat = lax.linalg.triangular_solve(eye + lmat, jnp.broadcast_to(eye, lmat.shape),
                                       left_side=True, lower=True, unit_diagonal=True)
    eg = jnp.exp(gc)[..., None]
    u_base = tmat @ (v * beta[..., None])
    w_dec = tmat @ (kb * eg)
    attn = jnp.einsum('bhnid,bhnjd->bhnij', q, k) * decay_causal
    q_dec = q * eg
    k_tail = k * jnp.exp(gc[..., -1:] - gc)[..., None]
    g_last = jnp.exp(gc[..., -1])

    def step(s, xs):
        u_b, w_c, a_c, q_c, kt_c, gl_c = xs
        u = u_b - w_c @ s
        o = q_c @ s + a_c @ u
        s = s * gl_c[..., None, None] + jnp.einsum('bhcd,bhce->bhde', kt_c, u)
        return s, o

    xs = tuple(jnp.moveaxis(a, 2, 0) for a in (u_base, w_dec, attn, q_dec, k_tail, g_last))
    s, o = lax.scan(step, s0.astype(f32), xs)
    o = jnp.moveaxis(o, 0, 2).reshape(b, h, n * c, dv)[:, :, :t]
    return jnp.moveaxis(o, 1, 2), s


def mixer_layer(x, c, conv_a_buf, conv_qkv_buf, s0, norm_g, w_ada, b_ada, w_in, conv_a_w,
                conv_qkv_w, a_log, dt_bias, o_norm_g, w_out):
    f32 = jnp.float32
    bsz, t, _ = x.shape
    mod = jax.nn.silu(c) @ w_ada + b_ada
    shift, scale, gate = jnp.split(mod[:, None, :], 3, axis=-1)
    h = rmsnorm(x, norm_g) * (1.0 + scale) + shift
    z = h @ w_in
    b_a, c_a, h_a, g_a, qkv, g_b, beta_logit, alpha_logit = jnp.split(z, SPLIT_POINTS, axis=-1)
    conv_out, new_a_buf = causal_depthwise_conv(c_a * h_a, conv_a_buf, conv_a_w)
    y_a = b_a * conv_out * jax.nn.silu(g_a)
    qkv_c, new_qkv_buf = causal_depthwise_conv(qkv, conv_qkv_buf, conv_qkv_w)
    qkv_c = jax.nn.silu(qkv_c).astype(f32)
    q, k, v = [a.reshape(bsz, t, GDN_HEADS, HEAD_DIM) for a in jnp.split(qkv_c, 3, axis=-1)]
    q, k = l2norm(q), l2norm(k)
    beta = jax.nn.sigmoid(beta_logit.astype(f32))
    g = -jnp.exp(a_log.astype(f32)) * jax.nn.softplus(alpha_logit.astype(f32) + dt_bias.astype(f32))
    o, s = gated_delta_rule(q, k, v, g, beta, s0)
    o = rmsnorm(o, o_norm_g).reshape(bsz, t, D_GDN).astype(x.dtype) * jax.nn.silu(g_b)
    y = jnp.concatenate([y_a, o], axis=-1) @ w_out
    return x + gate * y, new_a_buf, new_qkv_buf, s.astype(s0.dtype)


def trunk(x, c, conv_a0, conv_qkv0, ssm0, norm_g, w_ada, b_ada, w_in, conv_a_w, conv_qkv_w,
          a_log, dt_bias, o_norm_g, w_out, final_norm_g):
    new_a, new_qkv, new_s = [], [], []
    for i in range(DEPTH):
        x, a_buf, qkv_buf, s = mixer_layer(x, c, conv_a0[i], conv_qkv0[i], ssm0[i], norm_g[i],
                                           w_ada[i], b_ada[i], w_in[i], conv_a_w[i], conv_qkv_w[i],
                                           a_log[i], dt_bias[i], o_norm_g[i], w_out[i])
        new_a.append(a_buf)
        new_qkv.append(qkv_buf)
        new_s.append(s)
    return rmsnorm(x, final_norm_g), jnp.stack(new_a), jnp.stack(new_qkv), jnp.stack(new_s)


def setup_inputs(seed: int = 0) -> dict:
    key = jax.random.key(seed)
    ks = jax.random.split(key, 18)
    f32 = jnp.float32

    def nrm(k, shape, s):
        return jax.random.normal(k, shape, f32) * s

    x_prompt = nrm(ks[0], (BATCH, SEQ, D_MODEL), 1.0)
    x_sample = nrm(ks[1], (DEC_BATCH, DEC_SEQ, D_MODEL), 1.0)
    state_conv_a = nrm(ks[2], (DEPTH, DEC_BATCH, CONV_A_WIDTH - 1, D_CONV), 1.0)
    state_conv_qkv = nrm(ks[3], (DEPTH, DEC_BATCH, CONV_QKV_WIDTH - 1, 3 * D_GDN), 1.0)
    state_ssm = nrm(ks[4], (DEPTH, DEC_BATCH, GDN_HEADS, HEAD_DIM, HEAD_DIM), 0.1)
    c_prompt = nrm(ks[5], (BATCH, D_MODEL), 1.0)
    c_sample = nrm(ks[6], (DEC_BATCH, D_MODEL), 1.0)
    norm_g = 1.0 + nrm(ks[7], (DEPTH, D_MODEL), 0.02)
    w_ada = nrm(ks[8], (DEPTH, D_MODEL, 3 * D_MODEL), 0.5 * D_MODEL ** -0.5)
    b_ada = nrm(ks[9], (DEPTH, 3 * D_MODEL), 0.02)
    w_in = nrm(ks[10], (DEPTH, D_MODEL, D_IN_PROJ), D_MODEL ** -0.5)
    conv_a_w = nrm(ks[11], (DEPTH, CONV_A_WIDTH, D_CONV), CONV_A_WIDTH ** -0.5)
    conv_qkv_w = nrm(ks[12], (DEPTH, CONV_QKV_WIDTH, 3 * D_GDN), CONV_QKV_WIDTH ** -0.5)
    a_log = jnp.log(jax.random.uniform(ks[13], (DEPTH, GDN_HEADS), f32, 1.0, 16.0))
    dt = jnp.exp(jax.random.uniform(ks[14], (DEPTH, GDN_HEADS), f32, math.log(1e-3), math.log(1e-1)))
    dt_bias = dt + jnp.log(-jnp.expm1(-dt))
    o_norm_g = 1.0 + nrm(ks[15], (DEPTH, HEAD_DIM), 0.02)
    w_out = nrm(ks[16], (DEPTH, D_MODEL, D_MODEL), D_MODEL ** -0.5)
    final_norm_g = 1.0 + nrm(ks[17], (D_MODEL,), 0.02)
    return {'x_prompt': x_prompt, 'x_sample': x_sample, 'state_conv_a': state_conv_a,
            'state_conv_qkv': state_conv_qkv, 'state_ssm': state_ssm, 'c_prompt': c_prompt,
            'c_sample': c_sample, 'norm_g': norm_g, 'w_ada': w_ada, 'b_ada': b_ada, 'w_in': w_in,
            'conv_a_w': conv_a_w, 'conv_qkv_w': conv_qkv_w, 'a_log': a_log, 'dt_bias': dt_bias,
            'o_norm_g': o_norm_g, 'w_out': w_out, 'final_norm_g': final_norm_g}


def reference(x_prompt, x_sample, state_conv_a, state_conv_qkv, state_ssm, c_prompt, c_sample,
              norm_g, w_ada, b_ada, w_in, conv_a_w, conv_qkv_w, a_log, dt_bias, o_norm_g, w_out,
              final_norm_g):
    bp = x_prompt.shape[0]
    zeros_a = jnp.zeros((DEPTH, bp, CONV_A_WIDTH - 1, D_CONV), x_prompt.dtype)
    zeros_qkv = jnp.zeros((DEPTH, bp, CONV_QKV_WIDTH - 1, 3 * D_GDN), x_prompt.dtype)
    zeros_s = jnp.zeros((DEPTH, bp, GDN_HEADS, HEAD_DIM, HEAD_DIM), state_ssm.dtype)
    y_prompt, conv_a_p, conv_qkv_p, ssm_p = trunk(
        x_prompt, c_prompt, zeros_a, zeros_qkv, zeros_s, norm_g, w_ada, b_ada, w_in, conv_a_w,
        conv_qkv_w, a_log, dt_bias, o_norm_g, w_out, final_norm_g)
    y_sample, conv_a_s, conv_qkv_s, ssm_s = trunk(
        x_sample, c_sample, state_conv_a, state_conv_qkv, state_ssm, norm_g, w_ada, b_ada, w_in,
        conv_a_w, conv_qkv_w, a_log, dt_bias, o_norm_g, w_out, final_norm_g)
    return (y_prompt, y_sample, conv_a_p, conv_qkv_p, ssm_p, conv_a_s, conv_qkv_s, ssm_s)
```

```python
import numpy as np
from contextlib import ExitStack
import concourse.bass as bass
import concourse.mybir as mybir
from concourse.bass_utils import run_bass_kernel_spmd

F32 = mybir.dt.float32
BF16 = mybir.dt.bfloat16
F32R = mybir.dt.float32r
AF = mybir.ActivationFunctionType
ALU = mybir.AluOpType

D = 2048
KC = 16
DC = 1024
NH = 8
HD = 128
EPS = 1e-6
NEG = -30000.0
NS = 16
UNIT = 4096
NUNIT = 64


class Buf:
    __slots__ = ("w", "r")

    def __init__(self):
        self.w = None
        self.r = {}


class Tl:
    def __init__(self, h, b=None, psum=False):
        self.h = h
        self.b = b if b is not None else Buf()
        self.psum = psum

    def __getitem__(self, k):
        return self.h[k]


class Eng:
    def __init__(self, raw, sem, name):
        self.raw, self.sem, self.name = raw, sem, name
        self.cnt = 0
        self.known = {}

    def wait(self, tok):
        if tok is None:
            return
        sem, val = tok
        if self.name == "pe" and sem is self.sem:
            return
        if self.known.get(id(sem), 0) >= val:
            return
        self.raw.wait_ge(sem, val)
        self.known[id(sem)] = val

    def done(self, ins):
        self.cnt += 1
        ins.then_inc(self.sem, 1)
        return (self.sem, self.cnt)


class MK:
    def __init__(self, nc):
        self.nc = nc
        self.es = ExitStack()
        sm = lambda n: self.es.enter_context(nc.semaphore(n))
        self.pe = Eng(nc.tensor, sm("s_pe"), "pe")
        self.act = Eng(nc.scalar, sm("s_act"), "act")
        self.dve = Eng(nc.vector, sm("s_dve"), "dve")
        self.pool = Eng(nc.gpsimd, sm("s_pool"), "pool")
        self.sp = Eng(nc.sync, sm("s_sp"), "sp")
        self.rings = {"sp": [[sm(f"d_sp{i}"), 0] for i in range(24)],
                      "pool": [[sm(f"d_pl{i}"), 0] for i in range(12)],
                      "act": [[sm(f"d_ac{i}"), 0] for i in range(12)]}
        self.dcnt = {"sp": 0, "pool": 0, "act": 0}
        self.nalloc = 0

    def sb(self, shape, dt=F32, name=None):
        self.nalloc += 1
        return Tl(self.nc.alloc_sbuf_tensor(f"sb_{name}_{self.nalloc}", list(shape), dt))

    def ps(self, shape, dt=F32, name=None):
        self.nalloc += 1
        return Tl(self.nc.alloc_psum_tensor(f"ps_{name}_{self.nalloc}", list(shape), dt), psum=True)

    def _pre(self, eng, r, w):
        for t in r:
            eng.wait(t.b.w)
            if t.psum:
                for tok in list(t.b.r.values()):
                    if not (tok[0] is eng.sem):
                        eng.wait(tok)
        for t in w:
            eng.wait(t.b.w)
            for tok in list(t.b.r.values()):
                if not (tok[0] is eng.sem):
                    eng.wait(tok)

    def _post(self, tok, r, w):
        for t in r:
            t.b.r[id(tok[0])] = tok
        for t in w:
            t.b.w = tok
            t.b.r = {}

    def op(self, eng, fn, r=(), w=()):
        self._pre(eng, r, w)
        tok = eng.done(fn())
        self._post(tok, r, w)
        return tok

    def V(self, fn, r=(), w=()):
        return self.op(self.dve, fn, r, w)

    def A(self, fn, r=(), w=()):
        return self.op(self.act, fn, r, w)

    def P(self, fn, r=(), w=()):
        return self.op(self.pe, fn, r, w)

    def G(self, fn, r=(), w=()):
        return self.op(self.pool, fn, r, w)

    def dma(self, q, out, in_, r=(), w=()):
        eng = {"sp": self.sp, "pool": self.pool, "act": self.act}[q]
        ring = self.rings[q]
        slot = ring[self.dcnt[q] % len(ring)]
        self.dcnt[q] += 1
        if slot[1] > 0:
            eng.wait((slot[0], slot[1]))
        self._pre(eng, r, w)
        ins = eng.raw.dma_start(out=out, in_=in_)
        slot[1] += 16
        ins.then_inc(slot[0], 16)
        tok = (slot[0], slot[1])
        self._post(tok, r, w)
        return tok

    def finish(self):
        for q in ("sp", "pool", "act"):
            for slot in self.rings[q]:
                if slot[1] > 0:
                    self.sp.wait((slot[0], slot[1]))
        for e in (self.pe, self.act, self.dve, self.pool):
            if e.cnt > 0:
                self.sp.wait((e.sem, e.cnt))


STOP = None


def build_program(T, L, TP):
    assert T % TP == 0 and TP % 512 == 0
    NPASS = T // TP
    NB = TP // 512
    NCH = TP // 64
    NTT = TP // 128
    nc = bass.Bass("TRN2", target_bir_lowering=False)
    mk = MK(nc)
    V, A, P, G, dma = mk.V, mk.A, mk.P, mk.G, mk.dma
    vec, act, pe, pool = nc.vector, nc.scalar, nc.tensor, nc.gpsimd

    def din(name, shape):
        return nc.dram_tensor(name, list(shape), F32, kind="ExternalInput").ap()

    def dout(name, shape):
        return nc.dram_tensor(name, list(shape), F32, kind="ExternalOutput").ap()

    xp = din("xp", [T, D])
    xs = din("xs", [NS, D])
    cT = din("cT", [128, KC, 17])
    sca = din("sca", [L, NS, 2, DC])
    scq = din("scq", [L, NS, 3, 3 * DC])
    ssm = din("ssm", [L, NS, NH, HD, HD])
    wall = din("wall", [L, NUNIT, 128, UNIT])
    wba_d = din("wba", [L, 128, KC * 16])
    ng_d = din("ng", [L, 128, KC])
    bada_d = din("bada", [L, 128, 48])
    cwa_d = din("cwa", [L, 128, 8 * 3])
    cwq_d = din("cwq", [L, 128, 24 * 4])
    alog_d = din("alog", [L, 64, 8])
    dtb_d = din("dtb", [L, 64, 8])
    ong_d = din("ong", [L, 128, 1])
    fng_d = din("fng", [128, D])

    y_p = dout("y_p", [T, D])
    y_s = dout("y_s", [NS, D])
    ca_p = dout("ca_p", [L, 2, DC])
    cq_p = dout("cq_p", [L, 3, 3 * DC])
    ss_p = dout("ss_p", [L, NH, HD, HD])
    ca_s = dout("ca_s", [L, NS, 2, DC])
    cq_s = dout("cq_s", [L, NS, 3, 3 * DC])
    ss_s = dout("ss_s", [L, NS, NH, HD, HD])

    xscr = nc.dram_tensor("xscr", [T + NS, D], F32, kind="Internal").ap()
    gscr = nc.dram_tensor("gscr", [L, 17, D], F32, kind="Internal").ap()
    GSCR = Tl(gscr)

    TW = TP + NS

    hT = mk.sb([128, KC, TW], BF16, "hT")
    catT = mk.sb([128, KC, TW], BF16, "catT")
    hT_b = [Tl(hT.h) for _ in range(NB + 1)]
    cat_b = [Tl(catT.h) for _ in range(NB + 1)]
    wring = [mk.sb([128, UNIT], BF16, f"wr{i}") for i in range(4)]
    wba = [mk.sb([128, KC * 16], BF16, f"wba{i}") for i in range(2)]
    ident_f = mk.sb([128, 128], F32, "ident_f")
    ident_b = mk.sb([128, 128], BF16, "ident_b")
    ones_b = mk.sb([128, 128], BF16, "ones_b")
    ones_f = mk.sb([128, 128], F32, "ones_f")
    Umat = mk.sb([64, 64], F32, "Umat")
    M1 = mk.sb([64, 64], F32, "M1")
    M2 = mk.sb([64, 64], F32, "M2")
    M3 = mk.sb([64, 64], F32, "M3")
    cT_f = mk.sb([128, KC, 17], F32, "cT_f")
    scT = mk.sb([128, KC, 17], BF16, "scT")
    mod_sb = mk.sb([128, 48, 17], F32, "mod_sb")
    ng = mk.sb([128, KC], F32, "ng")
    bada = mk.sb([128, 48], F32, "bada")
    cwa = mk.sb([128, 8, 3], F32, "cwa")
    cwq = mk.sb([128, 24, 4], F32, "cwq")
    alog = mk.sb([64, 8], F32, "alog")
    dtb = mk.sb([64, 8], F32, "dtb")
    ong = mk.sb([128, 1], F32, "ong")
    gsP = mk.sb([128, KC], F32, "gsP")
    gsS = mk.sb([128, KC, NS], F32, "gsS")
    Sst = mk.sb([128, NH, HD], F32, "Sst")
    Sst_b = [Tl(Sst.h) for _ in range(NH)]
    b_beta = mk.sb([64, NCH, 8], F32, "b_beta")
    b_lnb = mk.sb([64, NCH, 8], F32, "b_lnb")
    b_g = mk.sb([64, NCH, 8], F32, "b_g")
    b_gc = mk.sb([64, NCH, 8], F32, "b_gc")
    b_gcb = mk.sb([64, NCH, 8], F32, "b_gcb")
    b_tmp = mk.sb([64, NCH, 8], F32, "b_tmp")
    nexpA = mk.sb([64, 8], F32, "nexpA")
    s_beta = mk.sb([NS, 8], F32, "s_beta")
    s_a = mk.sb([NS, 8], F32, "s_a")
    s_tmp = mk.sb([NS, 16], F32, "s_tmp")
    s_dg = mk.sb([NS, 8, NS], F32, "s_dg")
    sb_bc = mk.sb([128, 8, NS], F32, "sb_bc")
    sa_bc = mk.sb([128, 8, NS], F32, "sa_bc")
    histA = mk.sb([128, 8, 2], F32, "histA")
    histQ = mk.sb([128, 24, 3], F32, "histQ")
    tailA = mk.sb([128, 2, 8], F32, "tailA")
    tailQ = mk.sb([128, 3, 24], F32, "tailQ")
    histA_b = [Tl(histA.h) for _ in range(8)]
    histQ_b = [Tl(histQ.h) for _ in range(24)]

    Nij = mk.sb([64, 8, 64], F32, "Nij"); Nji = mk.sb([64, 8, 64], F32, "Nji")
    Pa = mk.sb([64, 8, 64], F32, "Pa"); PTa = mk.sb([64, 8, 64], F32, "PTa")
    attnT2 = [mk.sb([64, 8, 64], F32, f"attnT{i}") for i in range(2)]
    Xt2 = [mk.sb([64, 8, 64], F32, f"Xt{i}") for i in range(2)]
    KBE = mk.sb([64, 8, 128], F32, "KBE")
    KTl2 = [mk.sb([64, 8, 128], F32, f"KTl{i}") for i in range(2)]
    VB2 = [mk.sb([64, 8, 128], F32, f"VB{i}") for i in range(2)]
    zsamp = mk.sb([128, 64, NS], F32, "zsamp")
    zsamp_b = [Tl(zsamp.h) for _ in range(16)]
    gstate = {"n": 0}
    u_sb = [mk.sb([64, 128], F32, f"u_sb{i}") for i in range(2)]
    IP = [mk.ps([128, 512], F32, f"ip{i}") for i in range(4)]
    GB = [mk.ps([128, 512], F32, f"gb{i}") for i in range(4)]

    ARENA = (nc.sbuf_bytes_remaining - 1024) // 4
    print('arena words', ARENA)
    arena = nc.alloc_sbuf_tensor("arena", [128, ARENA], F32)
    ast = {"off": 0}

    def carve(shape, dt=F32, parts=None):
        n = int(np.prod(shape[1:]))
        words = n if dt == F32 else (n + 1) // 2
        o = ast["off"]
        assert o + words <= ARENA, ("arena overflow", o, words)
        ast["off"] = o + words
        ap = arena[0:shape[0], o:o + words]
        if dt != F32:
            ap = ap.bitcast(dt)[:, 0:n]
        if len(shape) == 3:
            ap = ap.rearrange("p (a b) -> p a b", b=shape[2])
        return Tl(ap)

    def barrier():
        engs = (mk.pe, mk.act, mk.dve, mk.pool, mk.sp)
        for e in engs:
            for o in engs:
                if o is not e and o.cnt > 0:
                    e.wait((o.sem, o.cnt))
            for slot in mk.rings["sp"] + mk.rings["act"]:
                if slot[1] > 0:
                    e.wait((slot[0], slot[1]))
        ast["off"] = 0

    EPSB = mk.sb([128, 1], F32, "EPSB")
    G(lambda: pool.memset(EPSB[:], EPS), w=[EPSB])
    XB = {}

    def xb(u, tile):
        if (u, tile) not in XB:
            XB[(u, tile)] = Tl(xscr)
        return XB[(u, tile)]

    G(lambda: pool.memset(ident_f[:], 1.0), w=[ident_f])
    G(lambda: pool.affine_select(out=ident_f[:], in_=ident_f[:], pattern=[[-1, 128]], compare_op=ALU.is_equal,
                                 fill=0.0, base=0, channel_multiplier=1), r=[ident_f], w=[ident_f])
    V(lambda: vec.tensor_copy(ident_b[:], ident_f[:]), r=[ident_f], w=[ident_b])
    G(lambda: pool.memset(ones_b[:], 1.0), w=[ones_b])
    G(lambda: pool.memset(ones_f[:], 1.0), w=[ones_f])
    G(lambda: pool.memset(Umat[:], 1.0), w=[Umat])
    G(lambda: pool.affine_select(out=Umat[:], in_=Umat[:], pattern=[[1, 64]], compare_op=ALU.is_ge,
                                 fill=0.0, base=0, channel_multiplier=-1), r=[Umat], w=[Umat])
    for Mx, pat, cm, cop in ((M1, 1, -1, ALU.is_ge), (M2, 1, -1, ALU.is_gt), (M3, -1, 1, ALU.is_gt)):
        G(lambda Mx=Mx: pool.memset(Mx[:], 0.0), w=[Mx])
        G(lambda Mx=Mx, pat=pat, cm=cm, cop=cop: pool.affine_select(
            out=Mx[:], in_=Mx[:], pattern=[[pat, 64]], compare_op=cop, fill=NEG, base=0, channel_multiplier=cm),
          r=[Mx], w=[Mx])
    dma("sp", cT_f[:], cT[:, :, :], w=[cT_f])
    A(lambda: act.activation(out=scT[:], in_=cT_f[:], func=AF.Silu), r=[cT_f], w=[scT])

    if STOP == 'c':
        mk.finish(); return nc
    wq = []
    for l in range(L):
        for u in range(24):
            wq.append((l, u))
        for ps_ in range(NPASS):
            for u in range(24, 64):
                wq.append((l, u))
    wstate = {"next": 0}

    def wload_next():
        i = wstate["next"]
        if i >= len(wq):
            return
        l, u = wq[i]
        slot = wring[i % 4]
        dma("pool", slot[:], wall[l, u, :, :], w=[slot])
        wstate["next"] = i + 1

    wcons = {"i": 0}

    def wslot():
        return wring[wcons["i"] % 4]

    def wdone():
        wcons["i"] += 1
        wload_next()

    for _ in range(4):
        wload_next()

    bc3 = lambda ap, shape: ap.to_broadcast(list(shape))
    RR = lambda ap: ap.bitcast(F32R)

    for l in range(L):
        last = (l == L - 1)
        wb = wba[l % 2]
        dma("pool", wb[:], wba_d[l, :, :], w=[wb])
        dma("sp", ng[:], ng_d[l, :, :], w=[ng])
        dma("sp", bada[:], bada_d[l, :, :], w=[bada])
        dma("sp", cwa[:], cwa_d[l, :, :].rearrange("p (j k) -> p j k", k=3), w=[cwa])
        dma("sp", cwq[:], cwq_d[l, :, :].rearrange("p (j k) -> p j k", k=4), w=[cwq])
        dma("sp", alog[:], alog_d[l, :, :], w=[alog])
        dma("sp", dtb[:], dtb_d[l, :, :], w=[dtb])
        dma("sp", ong[:], ong_d[l, :, :], w=[ong])
        dma("sp", ca_s[l, :, 0, :], sca[l, :, 1, :])
        dma("sp", cq_s[l, :, 0:2, :], scq[l, :, 1:3, :])

        for u in range(24):
            slot = wslot()
            for ct in range(2):
                t_ = u * 2 + ct
                bank = GB[t_ // 16]
                for kc in range(KC):
                    P(lambda bank=bank, t_=t_, ct=ct, kc=kc, slot=slot: pe.matmul(
                        bank[:, (t_ % 16) * 32:(t_ % 16) * 32 + 17],
                        slot[:, ct * 2048 + kc * 128: ct * 2048 + (kc + 1) * 128],
                        scT[:, kc, :], start=(kc == 0), stop=(kc == KC - 1)),
                      r=[slot, scT], w=[bank])
            wdone()
        for bi in range(3):
            V(lambda bi=bi: vec.tensor_tensor(
                mod_sb[:, bi * 16:(bi + 1) * 16, :],
                GB[bi][:, :].rearrange("p (a b) -> p a b", b=32)[:, :, 0:17],
                bc3(bada[:, bi * 16:(bi + 1) * 16, None], [128, 16, 17]), ALU.add),
              r=[GB[bi], bada], w=[mod_sb])
        V(lambda: vec.scalar_tensor_tensor(gsP[:], mod_sb[:, 16:32, 0], 1.0, ng[:], ALU.add, ALU.mult),
          r=[mod_sb, ng], w=[gsP])
        V(lambda: vec.scalar_tensor_tensor(gsS[:], mod_sb[:, 16:32, 1:17], 1.0,
                                           bc3(ng[:, :, None], [128, KC, NS]), ALU.add, ALU.mult),
          r=[mod_sb, ng], w=[gsS])
        barrier()
        gate_tok = carve([17, D])
        for kc in range(KC):
            bk = GB[kc // 4]
            P(lambda kc=kc, bk=bk: pe.transpose(bk[0:17, (kc % 4) * 128:(kc % 4 + 1) * 128],
                                                mod_sb[:, 32 + kc, :], ident_f[:]),
              r=[mod_sb, ident_f], w=[bk])
        for q4 in range(4):
            A(lambda q4=q4: act.copy(gate_tok[:, q4 * 512:(q4 + 1) * 512], GB[q4][0:17, :]),
              r=[GB[q4]], w=[gate_tok])
        dma("sp", gscr[l, :, :], gate_tok[:], r=[gate_tok], w=[GSCR])

        if STOP == 'mod':
            mk.finish(); return nc
        for ps_ in range(NPASS):
            t0 = ps_ * TP
            do_s = (ps_ == 0)
            ntile = NTT + (1 if do_s else 0)
            barrier()
            xt = [carve([128, D]) for _ in range(3)]
            xn = [carve([128, D], BF16) for _ in range(3)]
            stat = [carve([128, 4]) for _ in range(3)]
            tmpS = carve([128, KC, NS])
            for tt in range(ntile):
                is_s = (tt == NTT)
                np_ = NS if is_s else 128
                xti, xni, sti = xt[tt % 3], xn[tt % 3], stat[tt % 3]
                gt_ = (T // 128) if is_s else (t0 // 128 + tt)
                if is_s:
                    sap = xs[:, :] if l == 0 else xscr[T:T + NS, :]
                else:
                    sap = (xp if l == 0 else xscr)[t0 + tt * 128: t0 + (tt + 1) * 128, :]
                dma("sp", xti[0:np_, :], sap, r=([] if l == 0 else [xb(u_, gt_) for u_ in range(8)]), w=[xti])
                A(lambda: act.activation(out=xni[0:np_, :], in_=xti[0:np_, :], func=AF.Square,
                                         accum_out=sti[0:np_, 0:1]), r=[xti], w=[xni, sti])
                V(lambda: vec.tensor_scalar(sti[0:np_, 1:2], sti[0:np_, 0:1], 1.0 / D, EPS, ALU.mult, ALU.add),
                  r=[sti], w=[sti])
                A(lambda: act.activation(out=sti[0:np_, 2:3], in_=sti[0:np_, 1:2], func=AF.Ln), r=[sti], w=[sti])
                A(lambda: act.activation(out=sti[0:np_, 3:4], in_=sti[0:np_, 2:3], func=AF.Exp, scale=-0.5),
                  r=[sti], w=[sti])
                A(lambda: act.activation(out=xni[0:np_, :], in_=xti[0:np_, :], func=AF.Copy, scale=sti[0:np_, 3:4]),
                  r=[xti, sti], w=[xni])
                pb = (GB[0], GB[1]) if tt % 2 == 0 else (GB[2], GB[3])
                for kc in range(KC):
                    bk = pb[kc // 8]
                    P(lambda kc=kc, bk=bk: pe.transpose(
                        bk[:, :].bitcast(BF16)[:, (kc % 8) * 128:(kc % 8) * 128 + np_],
                        xni[0:np_, kc * 128:(kc + 1) * 128], ident_b[0:np_, 0:np_]),
                      r=[xni, ident_b], w=[bk])
                if not is_s:
                    blk = tt // 4
                    c0 = tt * 128
                    for kc in range(KC):
                        bk = pb[kc // 8]
                        src_ap = lambda kc=kc, bk=bk: bk[:, :].bitcast(BF16)[:, (kc % 8) * 128:(kc % 8 + 1) * 128]
                        if kc < 8:
                            A(lambda kc=kc, s=src_ap: act.activation(
                                out=hT[:, kc, c0:c0 + 128], in_=s(), func=AF.Identity,
                                scale=gsP[:, kc:kc + 1], bias=mod_sb[:, kc, 0:1]),
                              r=[bk, gsP, mod_sb], w=[hT_b[blk]])
                        else:
                            V(lambda kc=kc, s=src_ap: vec.tensor_scalar(
                                hT[:, kc, c0:c0 + 128], s(), gsP[:, kc:kc + 1], mod_sb[:, kc, 0:1],
                                ALU.mult, ALU.add),
                              r=[bk, gsP, mod_sb], w=[hT_b[blk]])
                else:
                    for half in range(2):
                        bk = pb[half]
                        V(lambda half=half, bk=bk: vec.tensor_tensor(
                            tmpS[:, half * 8:(half + 1) * 8, :],
                            bk[:, :].bitcast(BF16).rearrange("p (a b) -> p a b", b=128)[:, 0:8, 0:NS],
                            gsS[:, half * 8:(half + 1) * 8, :], ALU.mult),
                          r=[bk, gsS], w=[tmpS])
                    V(lambda: vec.tensor_tensor(hT[:, :, TP:TP + NS], tmpS[:], mod_sb[:, 0:16, 1:17], ALU.add),
                      r=[tmpS, mod_sb], w=[hT_b[NB]])

            if STOP == 'A':
                mk.finish(); return nc
            pba = GB[0]
            for n in range(NCH):
                for kc in range(KC):
                    P(lambda n=n, kc=kc: pe.matmul(pba[0:64, n * 16:(n + 1) * 16], hT[:, kc, n * 64:(n + 1) * 64],
                                                   wb[:, kc * 16:(kc + 1) * 16], start=(kc == 0), stop=(kc == KC - 1)),
                      r=[hT_b[n // 8], wb], w=[pba])
            pv = lambda sl: pba[0:64, 0:NCH * 16].rearrange("p (n c) -> p n c", c=16)[:, :, sl]
            W_ = [64, NCH, 8]
            A(lambda: act.activation(out=nexpA[:], in_=alog[:], func=AF.Exp), r=[alog], w=[nexpA])
            V(lambda: vec.tensor_scalar(nexpA[:], nexpA[:], -1.0, None, ALU.mult), r=[nexpA], w=[nexpA])
            A(lambda: act.activation(out=b_tmp[:], in_=pv(slice(0, 8)), func=AF.Exp, scale=-1.0), r=[pba], w=[b_tmp])
            A(lambda: act.activation(out=b_lnb[:], in_=b_tmp[:], func=AF.Ln, bias=1.0), r=[b_tmp], w=[b_lnb])
            A(lambda: act.activation(out=b_beta[:], in_=b_lnb[:], func=AF.Exp, scale=-1.0), r=[b_lnb], w=[b_beta])
            V(lambda: vec.tensor_tensor(b_tmp[:], pv(slice(8, 16)), bc3(dtb[:, None, :], W_), ALU.add),
              r=[pba, dtb, b_lnb], w=[b_tmp])
            A(lambda: act.activation(out=b_tmp[:], in_=b_tmp[:], func=AF.Exp), r=[b_tmp], w=[b_tmp])
            A(lambda: act.activation(out=b_tmp[:], in_=b_tmp[:], func=AF.Ln, bias=1.0), r=[b_tmp], w=[b_tmp])
            V(lambda: vec.tensor_tensor(b_g[:], b_tmp[:], bc3(nexpA[:, None, :], W_), ALU.mult),
              r=[b_tmp, nexpA], w=[b_g])
            pgc = GB[1]
            P(lambda: pe.matmul(pgc[0:64, 0:NCH * 8], Umat[:], b_g[:].rearrange("p n c -> p (n c)"),
                                start=True, stop=True), r=[Umat, b_g], w=[pgc])
            A(lambda: act.copy(b_gc[:].rearrange("p n c -> p (n c)"), pgc[0:64, 0:NCH * 8]), r=[pgc], w=[b_gc])
            V(lambda: vec.tensor_tensor(b_gcb[:], b_gc[:], b_lnb[:], ALU.subtract), r=[b_gc, b_lnb], w=[b_gcb])
            if do_s:
                psb = GB[2]
                for kc in range(KC):
                    P(lambda kc=kc: pe.matmul(psb[0:NS, 0:16], hT[:, kc, TP:TP + NS], wb[:, kc * 16:(kc + 1) * 16],
                                              start=(kc == 0), stop=(kc == KC - 1)), r=[hT_b[NB], wb], w=[psb])
                A(lambda: act.activation(out=s_tmp[:, 0:8], in_=psb[0:NS, 0:8], func=AF.Exp, scale=-1.0),
                  r=[psb], w=[s_tmp])
                A(lambda: act.activation(out=s_tmp[:, 0:8], in_=s_tmp[:, 0:8], func=AF.Ln, bias=1.0),
                  r=[s_tmp], w=[s_tmp])
                A(lambda: act.activation(out=s_beta[:], in_=s_tmp[:, 0:8], func=AF.Exp, scale=-1.0),
                  r=[s_tmp], w=[s_beta])
                V(lambda: vec.tensor_tensor(s_tmp[:, 8:16], psb[0:NS, 8:16], dtb[0:NS, :], ALU.add),
                  r=[psb, dtb], w=[s_tmp])
                A(lambda: act.activation(out=s_tmp[:, 8:16], in_=s_tmp[:, 8:16], func=AF.Exp), r=[s_tmp], w=[s_tmp])
                A(lambda: act.activation(out=s_tmp[:, 8:16], in_=s_tmp[:, 8:16], func=AF.Ln, bias=1.0),
                  r=[s_tmp], w=[s_tmp])
                V(lambda: vec.tensor_tensor(s_tmp[:, 8:16], s_tmp[:, 8:16], nexpA[0:NS, :], ALU.mult),
                  r=[s_tmp, nexpA], w=[s_tmp])
                A(lambda: act.activation(out=s_a[:], in_=s_tmp[:, 8:16], func=AF.Exp), r=[s_tmp], w=[s_a])
                for srcv, dst in ((s_beta, sb_bc), (s_a, sa_bc)):
                    V(lambda srcv=srcv: vec.tensor_tensor(
                        s_dg[:], bc3(ident_f[0:NS, None, 0:NS], [NS, 8, NS]), bc3(srcv[:, :, None], [NS, 8, NS]),
                        ALU.mult), r=[ident_f, srcv], w=[s_dg])
                    P(lambda: pe.matmul(GB[3][:, 0:8 * NS], ones_f[0:NS, :], s_dg[:].rearrange("p a b -> p (a b)"),
                                        start=True, stop=True), r=[ones_f, s_dg], w=[GB[3]])
                    A(lambda dst=dst: act.copy(dst[:].rearrange("p a b -> p (a b)"), GB[3][:, 0:8 * NS]),
                      r=[GB[3]], w=[dst])

            if STOP == 'BA':
                mk.finish(); return nc
            barrier()
            acc = [carve([128, 512]) for _ in range(3)]
            ca_sb, sg = acc[1], acc[2]
            pq = [carve([128, 3 + 512]) for _ in range(3)]
            rn = carve([128, 512]); sqb = carve([128, 512], BF16)
            qn = carve([128, 512], BF16); kn = carve([128, 512], BF16); vsb = carve([128, 512], BF16)
            sgate2 = [carve([128, 512]) for _ in range(2)]
            qdec2 = [carve([128, 512]) for _ in range(2)]
            wdn2 = [carve([128, 512]) for _ in range(2)]
            eg2 = [carve([128, 512]) for _ in range(2)]
            dg1 = carve([64, 8, 64]); dg2 = carve([64, 8, 64])
            a1, a2 = dg1, dg2
            a3 = carve([64, 8, 64])
            sc1 = carve([64, 8]); sc2 = carve([64, 8])
            ot = carve([128, 512]); rn2 = carve([128, 512])
            osq = Tl(rn2.h.bitcast(BF16)[:, 0:512], rn2.b)

            def inproj(slot, banks, blk):
                c0, hb = blk * 512, hT_b[blk]
                for ct in range(2):
                    for kc in range(KC):
                        P(lambda ct=ct, kc=kc: pe.matmul(
                            banks[ct][:, 0:512], slot[:, ct * 2048 + kc * 128: ct * 2048 + (kc + 1) * 128],
                            hT[:, kc, c0:c0 + 512], start=(kc == 0), stop=(kc == KC - 1)),
                          r=[slot, hb], w=[banks[ct]])

            def sample_inproj(sa, sb_, g):
                for slot_, base in ((sa, 0), (sb_, 2)):
                    for ct in range(2):
                        for kc in range(KC):
                            P(lambda ct=ct, kc=kc, slot_=slot_, base=base: pe.matmul(
                                GB[2][:, (base + ct) * NS:(base + ct + 1) * NS],
                                slot_[:, ct * 2048 + kc * 128: ct * 2048 + (kc + 1) * 128],
                                hT[:, kc, TP:TP + NS], start=(kc == 0), stop=(kc == KC - 1)),
                              r=[slot_, hT_b[NB]], w=[GB[2]])
                A(lambda: act.copy(zsamp[:, g * 4:(g + 1) * 4, :].rearrange("p a b -> p (a b)"), GB[2][:, 0:4 * NS]),
                  r=[GB[2]], w=[zsamp_b[g]])

            pend = {"gen": None}
            pend2 = {"gen": None}

            def inproj_gen(sa, sb_, blk):
                hb = hT_b[blk]
                c0 = blk * 512
                for slot_, banks in ((sa, (IP[0], IP[1])), (sb_, (IP[2], IP[3]))):
                    for ct in range(2):
                        for kc in range(KC):
                            P(lambda ct=ct, kc=kc, slot_=slot_, banks=banks: pe.matmul(
                                banks[ct][:, 0:512], slot_[:, ct * 2048 + kc * 128: ct * 2048 + (kc + 1) * 128],
                                hT[:, kc, c0:c0 + 512], start=(kc == 0), stop=(kc == KC - 1)),
                              r=[slot_, hb], w=[banks[ct]])
                            if kc % 2 == 1:
                                yield

            def _pump(pd, n):
                for _ in range(n):
                    if pd["gen"] is None:
                        return
                    try:
                        next(pd["gen"])
                    except StopIteration:
                        pd["gen"] = None
                        return

            def pump(n=1):
                _pump(pend, n)

            def pump2(n=1):
                _pump(pend2, n)

            def drain():
                while pend["gen"] is not None:
                    pump()

            def drain2():
                while pend2["gen"] is not None:
                    pump2()

            def rec_gen(hd, blk, par, lastblk):
                Sb, Sap = Sst_b[hd], Sst[:, hd, :]
                X_, VB_, KT_, AT_ = Xt2[par], VB2[par], KTl2[par], attnT2[par]
                wdn_, qd_, eg_, sgt_ = wdn2[par], qdec2[par], eg2[par], sgate2[par]
                B3 = GB[3]
                for c in range(8):
                    cs = slice(c * 64, (c + 1) * 64)
                    us = u_sb[c % 2]
                    pu = B3[0:64, 128:256]
                    po = B3[:, (c % 2) * 64:(c % 2) * 64 + 64]
                    pS = B3[:, 256:384]
                    P(lambda: pe.matmul(pu, RR(X_[:, c, :]), RR(VB_[:, c, :]), start=True, stop=False),
                      r=[X_, VB_], w=[B3])
                    P(lambda: pe.matmul(pu, wdn_[:, cs], Sap, start=False, stop=True), r=[wdn_, Sb], w=[B3])
                    A(lambda: act.copy(RR(us[:]), pu), r=[B3], w=[us])
                    yield
                    P(lambda: pe.matmul(po, Sap, qd_[:, cs], start=True, stop=False), r=[Sb, qd_], w=[B3])
                    P(lambda: pe.matmul(po, RR(us[:]), RR(AT_[:, c, :]), start=False, stop=True), r=[us, AT_], w=[B3])
                    P(lambda: pe.matmul(pS, RR(KT_[:, c, :]), RR(us[:]), start=True, stop=True), r=[KT_, us], w=[B3])
                    A(lambda: act.copy(ot[:, cs], po), r=[B3], w=[ot])
                    V(lambda: vec.scalar_tensor_tensor(Sap, Sap, eg_[:, c * 64 + 63: c * 64 + 64], pS, ALU.mult, ALU.add),
                      r=[Sb, eg_, B3], w=[Sb])
                    yield
                A(lambda: act.activation(out=osq[:], in_=ot[:], func=AF.Square), r=[ot], w=[osq])
                P(lambda: pe.matmul(B3[:], ones_b[:], osq[:], start=True, stop=True), r=[ones_b, osq], w=[B3])
                A(lambda: act.activation(out=rn2[:], in_=B3[:], func=AF.Ln, scale=1.0 / HD, bias=EPSB[:, 0:1]),
                  r=[B3, EPSB], w=[rn2])
                A(lambda: act.activation(out=rn2[:], in_=rn2[:], func=AF.Exp, scale=-0.5), r=[rn2], w=[rn2])
                yield
                V(lambda: vec.tensor_tensor(ot[:], ot[:], rn2[:], ALU.mult), r=[ot, rn2], w=[ot])
                V(lambda: vec.scalar_tensor_tensor(catT[:, 8 + hd, blk * 512:(blk + 1) * 512], ot[:],
                                                   ong[:, 0:1], sgt_[:], ALU.mult, ALU.mult),
                  r=[ot, ong, sgt_], w=[cat_b[blk]])
                if lastblk and ps_ == NPASS - 1:
                    dma("sp", ss_p[l, hd, :, :], Sap, r=[Sb])

            for j in range(8):
                s1 = wslot()
                s2 = wring[(wcons["i"] + 1) % 4]
                ch = Tl(pq[0].h[:, 0:514], pq[0].b)
                if ps_ == 0:
                    V(lambda: vec.memset(ch[:, 0:2], 0.0), w=[ch])
                else:
                    V(lambda j=j: vec.tensor_copy(ch[:, 0:2], histA[:, j, :]), r=[histA_b[j]], w=[ch])
                if do_s:
                    sample_inproj(s1, s2, j)
                for blk in range(NB):
                    inproj(s1, (IP[0], IP[1]), blk)
                    inproj(s2, (IP[2], IP[3]), blk)
                    A(lambda: act.copy(ca_sb[:], IP[0][:]), r=[IP[0]], w=[ca_sb])
                    V(lambda: vec.tensor_tensor(ch[:, 2:514], ca_sb[:], IP[1][:], ALU.mult), r=[ca_sb, IP[1]], w=[ch])
                    V(lambda j=j: vec.tensor_scalar(acc[0][:], ch[:, 2:514], cwa[:, j, 2:3], None, ALU.mult),
                      r=[ch, cwa], w=[acc[0]])
                    for tap in (1, 0):
                        V(lambda j=j, tap=tap: vec.scalar_tensor_tensor(
                            acc[0][:], ch[:, tap:tap + 512], cwa[:, j, tap:tap + 1], acc[0][:], ALU.mult, ALU.add),
                          r=[ch, cwa, acc[0]], w=[acc[0]])
                    A(lambda: act.activation(out=sg[:], in_=IP[3][:], func=AF.Silu), r=[IP[3]], w=[sg])
                    V(lambda: vec.tensor_tensor(acc[0][:], acc[0][:], IP[2][:], ALU.mult), r=[acc[0], IP[2]], w=[acc[0]])
                    V(lambda j=j, blk=blk: vec.tensor_tensor(catT[:, j, blk * 512:(blk + 1) * 512], acc[0][:], sg[:],
                                                             ALU.mult), r=[acc[0], sg], w=[cat_b[blk]])
                    if blk == NB - 1:
                        A(lambda j=j: act.copy(histA[:, j, :], ch[:, 512:514]), r=[ch], w=[histA_b[j]])
                        if ps_ == NPASS - 1:
                            A(lambda j=j: act.copy(tailA[:, :, j], ch[:, 512:514]), r=[ch], w=[tailA])
                    else:
                        A(lambda: act.copy(ch[:, 0:2], ch[:, 512:514]), r=[ch], w=[ch])
                wdone()
                wdone()

            for hd in range(NH):
                s1 = wslot()
                s2 = wring[(wcons["i"] + 1) % 4]
                Sb = Sst_b[hd]
                Sap = Sst[:, hd, :]
                if ps_ == 0:
                    V(lambda: vec.memset(Sap, 0.0), w=[Sb])
                hq = [histQ_b[tn * 8 + hd] for tn in range(3)]
                for tn in range(3):
                    if ps_ == 0:
                        V(lambda tn=tn: vec.memset(pq[tn][:, 0:3], 0.0), w=[pq[tn]])
                    else:
                        V(lambda tn=tn: vec.tensor_copy(pq[tn][:, 0:3], histQ[:, tn * 8 + hd, :]),
                          r=[hq[tn]], w=[pq[tn]])
                if do_s:
                    sample_inproj(s1, s2, 8 + hd)
                cw = lambda tn, tap: cwq[:, tn * 8 + hd, tap:tap + 1]
                for blk in range(NB):
                    par = gstate["n"] % 2
                    gstate["n"] += 1
                    Xt, VB, KTl, attnT = Xt2[par], VB2[par], KTl2[par], attnT2[par]
                    wdn, qdec, eg128, sgate = wdn2[par], qdec2[par], eg2[par], sgate2[par]
                    if pend["gen"] is None:
                        pend["gen"] = inproj_gen(s1, s2, blk)
                    drain()
                    if blk == NB - 1:
                        wdone()
                        wdone()
                    ch0 = blk * 8
                    sv = lambda t_: t_[:, ch0:ch0 + 8, hd]
                    for tn in range(3):
                        A(lambda tn=tn: act.copy(pq[tn][:, 3:515], IP[tn][:]), r=[IP[tn]], w=[pq[tn]])
                    A(lambda: act.activation(out=sgate[:], in_=IP[3][:], func=AF.Silu), r=[IP[3]], w=[sgate])
                    if blk < NB - 1:
                        pend["gen"] = inproj_gen(s1, s2, blk + 1)
                    elif hd < NH - 1:
                        pend["gen"] = inproj_gen(wslot(), wring[(wcons["i"] + 1) % 4], 0)
                    def conv_t(tn):
                        V(lambda: vec.tensor_scalar(acc[tn][:], pq[tn][:, 3:515], cw(tn, 3), None, ALU.mult),
                          r=[pq[tn], cwq], w=[acc[tn]])
                        for tap in (2, 1, 0):
                            V(lambda tap=tap: vec.scalar_tensor_tensor(
                                acc[tn][:], pq[tn][:, tap:tap + 512], cw(tn, tap), acc[tn][:], ALU.mult, ALU.add),
                              r=[pq[tn], cwq, acc[tn]], w=[acc[tn]])
                        if blk == NB - 1:
                            A(lambda: act.copy(histQ[:, tn * 8 + hd, :], pq[tn][:, 512:515]), r=[pq[tn]], w=[hq[tn]])
                            if ps_ == NPASS - 1:
                                A(lambda: act.copy(tailQ[:, :, tn * 8 + hd], pq[tn][:, 512:515]), r=[pq[tn]], w=[tailQ])
                        else:
                            A(lambda: act.copy(pq[tn][:, 0:3], pq[tn][:, 512:515]), r=[pq[tn]], w=[pq[tn]])

                    def silu_t(tn):
                        if tn < 2:
                            A(lambda: act.activation(out=acc[tn][:], in_=acc[tn][:], func=AF.Silu), r=[acc[tn]], w=[acc[tn]])
                        else:
                            A(lambda: act.activation(out=vsb[:], in_=acc[2][:], func=AF.Silu), r=[acc[2]], w=[vsb])

                    def l2norm_t(tn, dst, scl, bank):
                        A(lambda: act.activation(out=sqb[:], in_=acc[tn][:], func=AF.Square), r=[acc[tn]], w=[sqb])
                        P(lambda: pe.matmul(bank[:], ones_b[:], sqb[:], start=True, stop=True), r=[ones_b, sqb], w=[bank])
                        A(lambda: act.activation(out=rn[:], in_=bank[:], func=AF.Ln, bias=EPSB[:, 0:1]),
                          r=[bank, EPSB], w=[rn])
                        A(lambda: act.activation(out=rn[:], in_=rn[:], func=AF.Exp, scale=-0.5), r=[rn], w=[rn])
                        V(lambda: vec.scalar_tensor_tensor(dst[:], acc[tn][:], scl, rn[:], ALU.mult, ALU.mult),
                          r=[acc[tn], rn], w=[dst])

                    idb = bc3(ident_f[0:64, None, 0:64], [64, 8, 64])
                    V(lambda: vec.tensor_tensor(dg1[:], idb, bc3(sv(b_gc)[:, :, None], [64, 8, 64]), ALU.mult),
                      r=[ident_f, b_gc], w=[dg1])
                    V(lambda: vec.tensor_tensor(dg2[:], idb, bc3(sv(b_gcb)[:, :, None], [64, 8, 64]), ALU.mult),
                      r=[ident_f, b_gcb], w=[dg2])
                    fl = lambda t_: t_[:].rearrange("p a b -> p (a b)")
                    P(lambda: pe.matmul(GB[0][:], ones_f[0:64, :], fl(dg1), start=True, stop=True),
                      r=[ones_f, dg1], w=[GB[0]])
                    P(lambda: pe.matmul(GB[1][0:64, :], ones_f[0:64, 0:64], fl(dg2), start=True, stop=True),
                      r=[ones_f, dg2], w=[GB[1]])
                    pump(4); pump2()
                    conv_t(1)
                    silu_t(1)
                    pump(4); pump2()
                    R3 = lambda bk: bk[0:64, :].rearrange("p (a b) -> p a b", b=64)
                    gcb3 = bc3(sv(b_gc)[:, :, None], [64, 8, 64])
                    gcbb3 = bc3(sv(b_gcb)[:, :, None], [64, 8, 64])
                    msk = lambda M: bc3(M[:, None, :], [64, 8, 64])
                    V(lambda: vec.tensor_tensor(a1[:], R3(GB[0]), gcb3, ALU.subtract), r=[GB[0], b_gc], w=[a1])
                    V(lambda: vec.tensor_tensor(a1[:], a1[:], msk(M1), ALU.add), r=[a1, M1], w=[a1])
                    V(lambda: vec.tensor_tensor(a2[:], R3(GB[1]), gcb3, ALU.subtract), r=[GB[1], b_gc], w=[a2])
                    V(lambda: vec.tensor_tensor(a2[:], a2[:], msk(M2), ALU.add), r=[a2, M2], w=[a2])
                    V(lambda: vec.tensor_tensor(a3[:], gcbb3, R3(GB[0]), ALU.subtract), r=[GB[0], b_gcb], w=[a3])
                    V(lambda: vec.tensor_tensor(a3[:], a3[:], msk(M3), ALU.add), r=[a3, M3], w=[a3])
                    V(lambda: vec.tensor_tensor(sc2[:], R3(GB[0])[:, :, 63], sv(b_gc), ALU.subtract),
                      r=[GB[0], b_gc], w=[sc2])
                    A(lambda: act.activation(out=eg128[:], in_=GB[0][:], func=AF.Exp), r=[GB[0]], w=[eg128])
                    l2norm_t(1, kn, 1.0, GB[2])
                    pump(4); pump2()
                    deferred = [lambda: conv_t(0), lambda: conv_t(2),
                                lambda: (silu_t(2), silu_t(0)), lambda: l2norm_t(0, qn, HD ** -0.5, GB[1])]
                    for a_ in (a2, a3, a1):
                        A(lambda a_=a_: act.activation(out=a_[:], in_=a_[:], func=AF.Exp), r=[a_], w=[a_])
                    A(lambda: act.activation(out=sc1[:], in_=sv(b_gcb), func=AF.Exp), r=[b_gcb], w=[sc1])
                    A(lambda: act.activation(out=sc2[:], in_=sc2[:], func=AF.Exp), r=[sc2], w=[sc2])
                    pump2()
                    for c in range(8):
                        cs = slice(c * 64, (c + 1) * 64)
                        P(lambda cs=cs: pe.matmul(GB[2][0:64, cs], kn[:, cs], kn[:, cs], start=True, stop=True),
                          r=[kn], w=[GB[2]])
                    pump(8); pump2()
                    V(lambda: vec.scalar_tensor_tensor(RR(Nij[:]), R3(GB[2]), -1.0, a3[:], ALU.mult, ALU.mult),
                      r=[GB[2], a3], w=[Nij])
                    V(lambda: vec.scalar_tensor_tensor(RR(Nji[:]), R3(GB[2]), -1.0, a2[:], ALU.mult, ALU.mult),
                      r=[GB[2], a2], w=[Nji])
                    V(lambda: vec.tensor_tensor(RR(Xt[:]), Nji[:], idb, ALU.add), r=[Nji, ident_f], w=[Xt])
                    Pc, PTc = Nij, Nji

                    def xupd(Pn_):
                        for c in range(8):
                            cs = slice(c * 64, (c + 1) * 64)
                            P(lambda c=c, cs=cs: pe.matmul(GB[2][0:64, cs], RR(Pn_[:, c, :]), RR(Xt[:, c, :]),
                                                           start=True, stop=True), r=[Pn_, Xt], w=[GB[2]])

                    def xadd():
                        V(lambda: vec.tensor_tensor(RR(fl(Xt)), fl(Xt), GB[2][0:64, :], ALU.add), r=[Xt, GB[2]], w=[Xt])

                    for lev in range(5):
                        lastlev = (lev == 4)
                        for c in range(8):
                            cs = slice(c * 64, (c + 1) * 64)
                            P(lambda c=c, cs=cs, Pc=Pc, PTc=PTc: pe.matmul(GB[0][0:64, cs], RR(PTc[:, c, :]), RR(Pc[:, c, :]),
                                                                           start=True, stop=True),
                              r=[Pc, PTc], w=[GB[0]])
                            if not lastlev:
                                P(lambda c=c, cs=cs, Pc=Pc, PTc=PTc: pe.matmul(GB[1][0:64, cs], RR(Pc[:, c, :]),
                                                                               RR(PTc[:, c, :]), start=True, stop=True),
                                  r=[Pc, PTc], w=[GB[1]])
                        if lev >= 1:
                            xupd(Pc)
                        pump2()
                        Pn, PTn = (Pa, PTa) if lev % 2 == 0 else (Nij, Nji)
                        A(lambda Pn=Pn: act.copy(RR(fl(Pn)), GB[0][0:64, :]), r=[GB[0]], w=[Pn])
                        if not lastlev:
                            V(lambda PTn=PTn: vec.tensor_copy(RR(fl(PTn)), GB[1][0:64, :]), r=[GB[1]], w=[PTn])
                        if lev >= 1:
                            xadd()
                        pump(); pump2()
                        Pc, PTc = Pn, PTn
                        if lev < len(deferred):
                            deferred[lev]()
                    xupd(Pc)
                    pump2()
                    xadd()
                    for c in range(8):
                        cs = slice(c * 64, (c + 1) * 64)
                        P(lambda cs=cs: pe.matmul(GB[2][0:64, cs], kn[:, cs], qn[:, cs], start=True, stop=True),
                          r=[kn, qn], w=[GB[2]])
                    V(lambda: vec.tensor_tensor(RR(attnT[:]), R3(GB[2]), a1[:], ALU.mult), r=[GB[2], a1], w=[attnT])
                    V(lambda: vec.tensor_tensor(qdec[:], qn[:], eg128[:], ALU.mult), r=[qn, eg128], w=[qdec])
                    for c in range(8):
                        cs = slice(c * 64, (c + 1) * 64)
                        P(lambda c=c, cs=cs: pe.transpose(GB[0][0:64, :].bitcast(BF16)[:, c * 128:(c + 1) * 128],
                                                          kn[:, cs], ident_b[:]), r=[kn, ident_b], w=[GB[0]])
                        P(lambda c=c, cs=cs: pe.transpose(GB[1][0:64, :].bitcast(BF16)[:, c * 128:(c + 1) * 128],
                                                          vsb[:, cs], ident_b[:]), r=[vsb, ident_b], w=[GB[1]])
                    pump(2); pump2()
                    T3 = lambda bk: bk[0:64, :].bitcast(BF16).rearrange("p (a b) -> p a b", b=128)
                    s3 = lambda t_: bc3(t_[:, :, None], [64, 8, 128])
                    V(lambda: vec.tensor_tensor(RR(KBE[:]), T3(GB[0]), s3(sc1), ALU.mult), r=[GB[0], sc1], w=[KBE])
                    V(lambda: vec.tensor_tensor(RR(KTl[:]), T3(GB[0]), s3(sc2), ALU.mult), r=[GB[0], sc2], w=[KTl])
                    V(lambda: vec.tensor_tensor(RR(VB[:]), T3(GB[1]), bc3(sv(b_beta)[:, :, None], [64, 8, 128]), ALU.mult),
                      r=[GB[1], b_beta], w=[VB])
                    for c in range(8):
                        cs = slice(c * 64, (c + 1) * 64)
                        P(lambda c=c, cs=cs: pe.matmul(GB[2][:, cs], RR(KBE[:, c, :]), RR(Xt[:, c, :]), start=True, stop=True),
                          r=[KBE, Xt], w=[GB[2]])
                    pump(); pump2()
                    A(lambda: act.mul(wdn[:], GB[2][:], -1.0), r=[GB[2]], w=[wdn])
                    drain2()
                    pend2["gen"] = rec_gen(hd, blk, par, blk == NB - 1)
            drain()
            drain2()

            if do_s:
                barrier()
                ha_tok = carve([NS, 2, 128]); hist_as = carve([128, 2, NS])
                zc = carve([128, NS]); acc_c = carve([128, NS]); sg_c = carve([128, NS])
                strow_c = [carve([NS, 128]) for _ in range(2)]

                def head_scratch():
                    return (carve([NS, 9, 128]), carve([128, 9, NS]), [carve([128, NS]) for _ in range(3)],
                            carve([128, NS, 2]), carve([128, NS]), [carve([128, NS]) for _ in range(4)],
                            carve([128, NS]), carve([128, NS, 128]), carve([NS, 128]), carve([NS, 128]),
                            carve([NS, 8, 128]), [carve([NS, 128]) for _ in range(2)])

                def conv_gen():
                    strow_i = [0]
                    strow = strow_c
                    sg_s = sg_c
                    accs = [acc_c]
                    for j in range(8):
                        zb = zsamp_b[j]
                        zz = lambda i: zsamp[:, j * 4 + i, :]
                        dma("sp", ha_tok[:], sca[l, :, :, j * 128:(j + 1) * 128], w=[ha_tok])
                        for r_ in range(2):
                            yield
                            P(lambda r_=r_: pe.transpose(GB[3][:, r_ * NS:(r_ + 1) * NS], ha_tok[:, r_, :],
                                                         ident_f[0:NS, 0:NS]), r=[ha_tok, ident_f], w=[GB[3]])
                        A(lambda: act.copy(hist_as[:].rearrange("p a b -> p (a b)"), GB[3][:, 0:2 * NS]),
                          r=[GB[3]], w=[hist_as])
                        V(lambda: vec.tensor_tensor(zc[:], zz(0), zz(1), ALU.mult), r=[zb], w=[zc])
                        V(lambda j=j: vec.tensor_scalar(accs[0][:], zc[:], cwa[:, j, 2:3], None, ALU.mult),
                          r=[zc, cwa], w=[accs[0]])
                        for tap in (1, 0):
                            V(lambda j=j, tap=tap: vec.scalar_tensor_tensor(
                                accs[0][:], hist_as[:, tap, :], cwa[:, j, tap:tap + 1], accs[0][:],
                                ALU.mult, ALU.add), r=[hist_as, cwa, accs[0]], w=[accs[0]])
                        A(lambda: act.activation(out=sg_s[:], in_=zz(3), func=AF.Silu), r=[zb], w=[sg_s])
                        V(lambda: vec.tensor_tensor(accs[0][:], accs[0][:], zz(2), ALU.mult), r=[accs[0], zb], w=[accs[0]])
                        V(lambda j=j: vec.tensor_tensor(catT[:, j, TP:TP + NS], accs[0][:], sg_s[:], ALU.mult),
                          r=[accs[0], sg_s], w=[cat_b[NB]])
                        yield
                        P(lambda: pe.transpose(GB[3][0:NS, 128:256], zc[:], ident_f[:]), r=[zc, ident_f], w=[GB[3]])
                        sr = strow[strow_i[0] % 2]; strow_i[0] += 1
                        A(lambda sr=sr: act.copy(sr[:], GB[3][0:NS, 128:256]), r=[GB[3]], w=[sr])
                        dma("sp", ca_s[l, :, 1, j * 128:(j + 1) * 128], sr[:], r=[sr])

                def head_gen(heads, B, S):
                    hs_tok, hist_s, accs, kq_s, sg_s, t_s, sqs, Sin, k_tok, u_tok, KM, strow = S
                    Sout = Sin
                    strow_i = [0]
                    for hd in heads:
                        zb = zsamp_b[8 + hd]
                        zz = lambda i: zsamp[:, (8 + hd) * 4 + i, :]
                        cw = lambda tn, tap: cwq[:, tn * 8 + hd, tap:tap + 1]
                        for tn in range(3):
                            dma("sp", hs_tok[:, tn * 3:(tn + 1) * 3, :],
                                scq[l, :, :, tn * DC + hd * 128: tn * DC + (hd + 1) * 128], w=[hs_tok])
                        dma("sp", Sin[:], ssm[l, :, hd, :, :].rearrange("r k v -> k r v"), w=[Sin])
                        for i9 in range(9):
                            yield
                            P(lambda i9=i9: pe.transpose(B[3][:, i9 * NS:(i9 + 1) * NS], hs_tok[:, i9, :],
                                                         ident_f[0:NS, 0:NS]), r=[hs_tok, ident_f], w=[B[3]])
                        A(lambda: act.copy(hist_s[:].rearrange("p a b -> p (a b)"), B[3][:, 0:9 * NS]),
                          r=[B[3]], w=[hist_s])
                        for tn in range(3):
                            V(lambda tn=tn: vec.tensor_scalar(accs[tn][:], zz(tn), cw(tn, 3), None, ALU.mult),
                              r=[zb, cwq], w=[accs[tn]])
                            for tap in (2, 1, 0):
                                V(lambda tn=tn, tap=tap: vec.scalar_tensor_tensor(
                                    accs[tn][:], hist_s[:, tn * 3 + tap, :], cw(tn, tap), accs[tn][:],
                                    ALU.mult, ALU.add), r=[hist_s, cwq, accs[tn]], w=[accs[tn]])
                            A(lambda tn=tn: act.activation(out=accs[tn][:], in_=accs[tn][:], func=AF.Silu),
                              r=[accs[tn]], w=[accs[tn]])
                            yield
                            P(lambda tn=tn: pe.transpose(B[3][0:NS, tn * 128:(tn + 1) * 128], zz(tn), ident_f[:]),
                              r=[zb, ident_f], w=[B[3]])
                            sr = strow[strow_i[0] % 2]; strow_i[0] += 1
                            A(lambda tn=tn, sr=sr: act.copy(sr[:], B[3][0:NS, tn * 128:(tn + 1) * 128]),
                              r=[B[3]], w=[sr])
                            dma("sp", cq_s[l, :, 2, tn * DC + hd * 128: tn * DC + (hd + 1) * 128], sr[:], r=[sr])
                        A(lambda: act.activation(out=sg_s[:], in_=zz(3), func=AF.Silu), r=[zb], w=[sg_s])
                        for tn, slot_i, scl in ((0, 1, HD ** -0.5), (1, 0, 1.0)):
                            V(lambda tn=tn: vec.tensor_tensor(sqs[:], accs[tn][:], accs[tn][:], ALU.mult),
                              r=[accs[tn]], w=[sqs])
                            yield
                            P(lambda: pe.matmul(B[0][:, 0:NS], ones_f[:], sqs[:], start=True, stop=True),
                              r=[ones_f, sqs], w=[B[0]])
                            A(lambda: act.activation(out=t_s[0][:], in_=B[0][:, 0:NS], func=AF.Ln, bias=EPSB[:, 0:1]),
                              r=[B[0], EPSB], w=[t_s[0]])
                            A(lambda: act.activation(out=t_s[0][:], in_=t_s[0][:], func=AF.Exp, scale=-0.5),
                              r=[t_s[0]], w=[t_s[0]])
                            V(lambda tn=tn, slot_i=slot_i, scl=scl: vec.scalar_tensor_tensor(
                                kq_s[:, :, slot_i], accs[tn][:], scl, t_s[0][:], ALU.mult, ALU.mult),
                              r=[accs[tn], t_s[0]], w=[kq_s])
                        for r_ in range(NS):
                            yield
                            P(lambda r_=r_: pe.matmul(B[0][:, 64 + r_ * 2: 64 + r_ * 2 + 2], Sin[:, r_, :],
                                                      kq_s[:, r_, :], start=True, stop=True),
                              r=[Sin, kq_s], w=[B[0]])
                        skq = lambda i: B[0][:, 64:64 + 2 * NS].rearrange("p (r c) -> p r c", c=2)[:, :, i]
                        abc, bbc = sa_bc[:, hd, :], sb_bc[:, hd, :]
                        u_ = t_s[1]
                        V(lambda: vec.tensor_tensor(u_[:], skq(0), abc, ALU.mult), r=[B[0], sa_bc], w=[u_])
                        V(lambda: vec.tensor_tensor(u_[:], accs[2][:], u_[:], ALU.subtract), r=[accs[2], u_], w=[u_])
                        V(lambda: vec.tensor_tensor(u_[:], u_[:], bbc, ALU.mult), r=[u_, sb_bc], w=[u_])
                        V(lambda: vec.tensor_tensor(sqs[:], kq_s[:, :, 0], kq_s[:, :, 1], ALU.mult), r=[kq_s], w=[sqs])
                        yield
                        P(lambda: pe.matmul(B[1][:, 0:NS], ones_f[:], sqs[:], start=True, stop=True),
                          r=[ones_f, sqs], w=[B[1]])
                        o_ = t_s[2]
                        V(lambda: vec.tensor_tensor(o_[:], skq(1), abc, ALU.mult), r=[B[0], sa_bc], w=[o_])
                        V(lambda: vec.tensor_tensor(t_s[3][:], u_[:], B[1][:, 0:NS], ALU.mult), r=[u_, B[1]], w=[t_s[3]])
                        V(lambda: vec.tensor_tensor(o_[:], o_[:], t_s[3][:], ALU.add), r=[o_, t_s[3]], w=[o_])
                        V(lambda: vec.tensor_tensor(sqs[:], o_[:], o_[:], ALU.mult), r=[o_], w=[sqs])
                        yield
                        P(lambda: pe.matmul(B[1][:, 32:32 + NS], ones_f[:], sqs[:], start=True, stop=True),
                          r=[ones_f, sqs], w=[B[1]])
                        A(lambda: act.activation(out=t_s[3][:], in_=B[1][:, 32:32 + NS], func=AF.Ln,
                                                 scale=1.0 / HD, bias=EPSB[:, 0:1]), r=[B[1], EPSB], w=[t_s[3]])
                        A(lambda: act.activation(out=t_s[3][:], in_=t_s[3][:], func=AF.Exp, scale=-0.5),
                          r=[t_s[3]], w=[t_s[3]])
                        V(lambda: vec.tensor_tensor(o_[:], o_[:], t_s[3][:], ALU.mult), r=[o_, t_s[3]], w=[o_])
                        V(lambda: vec.scalar_tensor_tensor(catT[:, 8 + hd, TP:TP + NS], o_[:], ong[:, 0:1], sg_s[:],
                                                           ALU.mult, ALU.mult), r=[o_, ong, sg_s], w=[cat_b[NB]])
                        yield
                        P(lambda: pe.transpose(B[1][0:NS, 128:256], kq_s[:, :, 0], ident_f[:]),
                          r=[kq_s, ident_f], w=[B[1]])
                        yield
                        P(lambda: pe.transpose(B[1][0:NS, 256:384], u_[:], ident_f[:]), r=[u_, ident_f], w=[B[1]])
                        A(lambda: act.copy(k_tok[:], B[1][0:NS, 128:256]), r=[B[1]], w=[k_tok])
                        A(lambda: act.copy(u_tok[:], B[1][0:NS, 256:384]), r=[B[1]], w=[u_tok])
                        for r_ in range(NS):
                            if r_ % 8 == 0:
                                V(lambda r_=r_: vec.tensor_tensor(
                                    KM[:], bc3(k_tok[:, None, :], [NS, 8, 128]),
                                    bc3(ident_f[0:NS, r_:r_ + 8, None], [NS, 8, 128]), ALU.mult),
                                  r=[k_tok, ident_f], w=[KM])
                            bk = B[2 + (r_ // 4) % 2]
                            yield
                            P(lambda r_=r_, bk=bk: pe.matmul(bk[:, (r_ % 4) * 128:(r_ % 4 + 1) * 128], KM[:, r_ % 8, :],
                                                             u_tok[:], start=True, stop=True),
                              r=[KM, u_tok], w=[bk])
                            V(lambda r_=r_, bk=bk: vec.scalar_tensor_tensor(
                                Sout[:, r_, :], Sin[:, r_, :], sa_bc[:, hd, r_:r_ + 1],
                                bk[:, (r_ % 4) * 128:(r_ % 4 + 1) * 128], ALU.mult, ALU.add),
                              r=[Sin, sa_bc, bk], w=[Sout])
                        dma("sp", ss_s[l, :, hd, :, :].rearrange("r k v -> k r v"), Sout[:], r=[Sout])


                for _ in conv_gen():
                    pass
                gens = [head_gen([0, 2, 4, 6], GB, head_scratch()), head_gen([1, 3, 5, 7], IP, head_scratch())]
                while gens:
                    for g_ in list(gens):
                        try:
                            next(g_)
                        except StopIteration:
                            gens.remove(g_)
            if STOP in ('gdn', 'gdn_p', 'gdn_s'):
                mk.finish(); return nc
            if ps_ == NPASS - 1:
                tailo = carve([128, 128])
                P(lambda: pe.transpose(GB[0][0:16, 0:128], tailA[:].rearrange("p r j -> p (r j)"), ident_f[:]),
                  r=[tailA, ident_f], w=[GB[0]])
                A(lambda: act.copy(tailo[0:16, :], GB[0][0:16, 0:128]), r=[GB[0]], w=[tailo])
                dma("sp", ca_p[l, :, :].rearrange("r (j p) -> (r j) p", p=128), tailo[0:16, :], r=[tailo])
                P(lambda: pe.transpose(GB[0][0:72, 128:256], tailQ[:].rearrange("p r j -> p (r j)"), ident_f[:]),
                  r=[tailQ, ident_f], w=[GB[0]])
                tailo2 = carve([128, 128])
                A(lambda: act.copy(tailo2[0:72, :], GB[0][0:72, 128:256]), r=[GB[0]], w=[tailo2])
                dma("sp", cq_p[l, :, :].rearrange("r (j p) -> (r j) p", p=128), tailo2[0:72, :], r=[tailo2])

            barrier()
            NR = 8
            xc = [carve([128, 256]) for _ in range(NR)]
            gp = [carve([128, 256]) for _ in range(2)]
            gsm = [carve([NS, 256]) for _ in range(2)]
            yt = [carve([128, 256]) for _ in range(NR)]
            c_has_s = (ps_ == NPASS - 1) if NPASS >= 2 else do_s
            ntile_c = NTT + (1 if c_has_s else 0)
            for u in range(8):
                slot = wslot()
                gpi, gsi = gp[u % 2], gsm[u % 2]
                dma("sp", gpi[:], gscr[l, 0:1, u * 256:(u + 1) * 256].to_broadcast([128, 256]), r=[GSCR], w=[gpi])
                if c_has_s:
                    dma("sp", gsi[:], gscr[l, 1:17, u * 256:(u + 1) * 256], r=[GSCR], w=[gsi])
                for tt in range(ntile_c):
                    is_s = (tt == NTT)
                    np_ = NS if is_s else 128
                    c0 = TP if is_s else tt * 128
                    cb = cat_b[NB] if is_s else cat_b[tt // 4]
                    k_ = (u * ntile_c + tt)
                    bank = IP[k_ % 4]
                    xci, yti = xc[k_ % NR], yt[k_ % NR]
                    if is_s:
                        gt_ = T // 128
                        sap = (xs[:, u * 256:(u + 1) * 256] if l == 0 else xscr[T:T + NS, u * 256:(u + 1) * 256])
                        dap = xscr[T:T + NS, u * 256:(u + 1) * 256]
                    else:
                        r0 = t0 + tt * 128
                        gt_ = r0 // 128
                        sap = (xp if l == 0 else xscr)[r0:r0 + 128, u * 256:(u + 1) * 256]
                        dap = xscr[r0:r0 + 128, u * 256:(u + 1) * 256]
                    dstT = xb(u, gt_)
                    dma("sp", xci[0:np_, :], sap, r=([] if l == 0 else [dstT]), w=[xci])
                    for kc in range(KC):
                        P(lambda kc=kc, bank=bank: pe.matmul(bank[0:np_, 0:256], catT[:, kc, c0:c0 + np_],
                                                             slot[:, kc * 256:(kc + 1) * 256],
                                                             start=(kc == 0), stop=(kc == KC - 1)),
                          r=[cb, slot], w=[bank])
                    gt = gsi if is_s else gpi
                    V(lambda bank=bank, gt=gt, yti=yti: vec.tensor_tensor(yti[0:np_, :], bank[0:np_, 0:256],
                                                                          gt[0:np_, :], ALU.mult),
                      r=[bank, gt], w=[yti])
                    V(lambda xci=xci, yti=yti: vec.tensor_tensor(yti[0:np_, :], yti[0:np_, :], xci[0:np_, :], ALU.add),
                      r=[yti, xci], w=[yti])
                    dma("act", dap, yti[0:np_, :], r=[yti], w=[dstT])
                wdone()

    if STOP == 'C':
        mk.finish(); return nc
    barrier()
    xt = [carve([128, D]) for _ in range(2)]
    xn = [carve([128, D], BF16) for _ in range(2)]
    stat = [carve([128, 4]) for _ in range(2)]
    fng = carve([128, D])
    dma("sp", fng[:], fng_d[:, :], w=[fng])
    for tt in range(T // 128 + 1):
        is_s = (tt == T // 128)
        np_ = NS if is_s else 128
        xti, sti = xt[tt % 2], stat[tt % 2]
        junk = xn[tt % 2]
        rb = [xb(u_, tt) for u_ in range(8)]
        if is_s:
            dma("sp", xti[0:np_, :], xscr[T:T + NS, :], r=rb, w=[xti])
        else:
            dma("sp", xti[0:np_, :], xscr[tt * 128:(tt + 1) * 128, :], r=rb, w=[xti])
        A(lambda: act.activation(out=junk[0:np_, :], in_=xti[0:np_, :], func=AF.Square, accum_out=sti[0:np_, 0:1]),
          r=[xti], w=[junk, sti])
        V(lambda: vec.tensor_scalar(sti[0:np_, 1:2], sti[0:np_, 0:1], 1.0 / D, EPS, ALU.mult, ALU.add), r=[sti], w=[sti])
        A(lambda: act.activation(out=sti[0:np_, 2:3], in_=sti[0:np_, 1:2], func=AF.Ln), r=[sti], w=[sti])
        A(lambda: act.activation(out=sti[0:np_, 3:4], in_=sti[0:np_, 2:3], func=AF.Exp, scale=-0.5), r=[sti], w=[sti])
        V(lambda: vec.scalar_tensor_tensor(xti[0:np_, :], xti[0:np_, :], sti[0:np_, 3:4], fng[0:np_, :],
                                           ALU.mult, ALU.mult), r=[xti, sti, fng], w=[xti])
        if is_s:
            dma("sp", y_s[:, :], xti[0:np_, :], r=[xti])
        else:
            dma("sp", y_p[tt * 128:(tt + 1) * 128, :], xti[0:np_, :], r=[xti])
    mk.finish()
    return nc


def _prep_shared(inp, L):
    f = np.float32
    w_in, w_ada, w_out = inp["w_in"], inp["w_ada"], inp["w_out"]
    wall = np.empty((L, NUNIT, 128, UNIT), f)

    def coltile(w, c0):
        return w[:, c0:c0 + 128].reshape(KC, 128, 128).transpose(1, 0, 2)

    for l in range(L):
        for u in range(24):
            for ct in range(2):
                wall[l, u, :, ct * 2048:(ct + 1) * 2048] = coltile(w_ada[l], (u * 2 + ct) * 128).reshape(128, 2048)
        u = 24
        for j in range(8):
            for pair in ((1, 2), (0, 3)):
                for ct, grp in enumerate(pair):
                    wall[l, u, :, ct * 2048:(ct + 1) * 2048] = coltile(w_in[l], grp * DC + j * 128).reshape(128, 2048)
                u += 1
        for hd in range(8):
            cols = [4 * DC + hd * 128, 5 * DC + hd * 128, 6 * DC + hd * 128, 7 * DC + hd * 128]
            for pair in ((0, 1), (2, 3)):
                for ct, ci in enumerate(pair):
                    wall[l, u, :, ct * 2048:(ct + 1) * 2048] = coltile(w_in[l], cols[ci]).reshape(128, 2048)
                u += 1
        for g in range(8):
            wall[l, u] = w_out[l][:, g * 256:(g + 1) * 256].reshape(KC, 128, 256).transpose(1, 0, 2).reshape(128, UNIT)
            u += 1
    sh = {"wall": wall}
    sh["wba"] = np.ascontiguousarray(
        w_in[:L, :, 8 * DC:8 * DC + 16].reshape(L, KC, 128, 16).transpose(0, 2, 1, 3).reshape(L, 128, KC * 16))
    sh["ng"] = np.ascontiguousarray(inp["norm_g"][:L].reshape(L, KC, 128).transpose(0, 2, 1))
    sh["bada"] = np.ascontiguousarray(inp["b_ada"][:L].reshape(L, 48, 128).transpose(0, 2, 1))
    sh["cwa"] = np.ascontiguousarray(inp["conv_a_w"][:L].reshape(L, 3, 8, 128).transpose(0, 3, 2, 1).reshape(L, 128, 24))
    sh["cwq"] = np.ascontiguousarray(inp["conv_qkv_w"][:L].reshape(L, 4, 24, 128).transpose(0, 3, 2, 1).reshape(L, 128, 96))
    sh["alog"] = np.ascontiguousarray(np.broadcast_to(inp["a_log"][:L, None, :], (L, 64, 8)))
    sh["dtb"] = np.ascontiguousarray(np.broadcast_to(inp["dt_bias"][:L, None, :], (L, 64, 8)))
    sh["ong"] = np.ascontiguousarray(inp["o_norm_g"][:L].reshape(L, 128, 1))
    sh["fng"] = np.ascontiguousarray(np.broadcast_to(inp["final_norm_g"][None, :], (128, D)))
    return sh


_CACHE = {}


def run(inp, T, L, TP, n_cores=8):
    key = (T, L, TP)
    if key not in _CACHE:
        _CACHE[key] = build_program(T, L, TP)
    nc = _CACHE[key]
    f = np.float32
    sh = _prep_shared({k: np.asarray(v, f) for k, v in inp.items()}, L)
    B = inp["x_prompt"].shape[0]
    in_maps = []
    for c in range(n_cores):
        b = c % B
        rows = slice(c * NS, (c + 1) * NS)
        cc = np.concatenate([inp["c_prompt"][b:b + 1], inp["c_sample"][rows]], axis=0)
        m = dict(sh)
        m["xp"] = np.ascontiguousarray(inp["x_prompt"][b, :T], f)
        m["xs"] = np.ascontiguousarray(inp["x_sample"][rows, 0, :], f)
        m["cT"] = np.ascontiguousarray(cc.reshape(17, KC, 128).transpose(2, 1, 0), f)
        m["sca"] = np.ascontiguousarray(inp["state_conv_a"][:L, rows], f)
        m["scq"] = np.ascontiguousarray(inp["state_conv_qkv"][:L, rows], f)
        m["ssm"] = np.ascontiguousarray(inp["state_ssm"][:L, rows], f)
        in_maps.append(m)
    res = run_bass_kernel_spmd(nc, in_maps, core_ids=list(range(n_cores)))
    R = res.results
    nb = min(B, n_cores)
    y_p = np.stack([R[b]["y_p"] for b in range(nb)])
    ca_p = np.stack([R[b]["ca_p"] for b in range(nb)], axis=1)
    cq_p = np.stack([R[b]["cq_p"] for b in range(nb)], axis=1)
    ss_p = np.stack([R[b]["ss_p"] for b in range(nb)], axis=1)
    y_s = np.concatenate([R[c]["y_s"] for c in range(n_cores)], axis=0)[:, None, :]
    ca_s = np.concatenate([R[c]["ca_s"] for c in range(n_cores)], axis=1)
    cq_s = np.concatenate([R[c]["cq_s"] for c in range(n_cores)], axis=1)
    ss_s = np.concatenate([R[c]["ss_s"] for c in range(n_cores)], axis=1)
    return tuple(np.ascontiguousarray(a, dtype=f) for a in (y_p, y_s, ca_p, cq_p, ss_p, ca_s, cq_s, ss_s))


def kernel(**inputs):
    inp = {k: np.asarray(v) for k, v in inputs.items()}
    return run(inp, T=2048, L=4, TP=1024, n_cores=8)
```

```python
import numpy as np
from contextlib import ExitStack
import concourse.bass as bass
import concourse.mybir as mybir
from concourse.bass_utils import run_bass_kernel_spmd

F32 = mybir.dt.float32
BF16 = mybir.dt.bfloat16
F32R = mybir.dt.float32r
AF = mybir.ActivationFunctionType
ALU = mybir.AluOpType

D = 2048
KC = 16
DC = 1024
NH = 8
HD = 128
EPS = 1e-6
NEG = -30000.0
NS = 16
UNIT = 4096
NUNIT = 64


class Buf:
    __slots__ = ("w", "r")

    def __init__(self):
        self.w = None
        self.r = {}


class Tl:
    def __init__(self, h, b=None, psum=False):
        self.h = h
        self.b = b if b is not None else Buf()
        self.psum = psum

    def __getitem__(self, k):
        return self.h[k]


class Eng:
    def __init__(self, raw, sem, name):
        self.raw, self.sem, self.name = raw, sem, name
        self.cnt = 0
        self.known = {}

    def wait(self, tok):
        if tok is None:
            return
        sem, val = tok
        if self.name == "pe" and sem is self.sem:
            return
        if self.known.get(id(sem), 0) >= val:
            return
        self.raw.wait_ge(sem, val)
        self.known[id(sem)] = val

    def done(self, ins):
        self.cnt += 1
        ins.then_inc(self.sem, 1)
        return (self.sem, self.cnt)


class MK:
    def __init__(self, nc):
        self.nc = nc
        self.es = ExitStack()
        sm = lambda n: self.es.enter_context(nc.semaphore(n))
        self.pe = Eng(nc.tensor, sm("s_pe"), "pe")
        self.act = Eng(nc.scalar, sm("s_act"), "act")
        self.dve = Eng(nc.vector, sm("s_dve"), "dve")
        self.pool = Eng(nc.gpsimd, sm("s_pool"), "pool")
        self.sp = Eng(nc.sync, sm("s_sp"), "sp")
        self.rings = {"sp": [[sm(f"d_sp{i}"), 0] for i in range(24)],
                      "pool": [[sm(f"d_pl{i}"), 0] for i in range(12)],
                      "act": [[sm(f"d_ac{i}"), 0] for i in range(12)]}
        self.dcnt = {"sp": 0, "pool": 0, "act": 0}
        self.nalloc = 0

    def sb(self, shape, dt=F32, name=None):
        self.nalloc += 1
        return Tl(self.nc.alloc_sbuf_tensor(f"sb_{name}_{self.nalloc}", list(shape), dt))

    def ps(self, shape, dt=F32, name=None):
        self.nalloc += 1
        return Tl(self.nc.alloc_psum_tensor(f"ps_{name}_{self.nalloc}", list(shape), dt), psum=True)

    def _pre(self, eng, r, w):
        for t in r:
            eng.wait(t.b.w)
            if t.psum:
                for tok in list(t.b.r.values()):
                    if not (tok[0] is eng.sem):
                        eng.wait(tok)
        for t in w:
            eng.wait(t.b.w)
            for tok in list(t.b.r.values()):
                if not (tok[0] is eng.sem):
                    eng.wait(tok)

    def _post(self, tok, r, w):
        for t in r:
            t.b.r[id(tok[0])] = tok
        for t in w:
            t.b.w = tok
            t.b.r = {}

    def op(self, eng, fn, r=(), w=()):
        self._pre(eng, r, w)
        tok = eng.done(fn())
        self._post(tok, r, w)
        return tok

    def V(self, fn, r=(), w=()):
        return self.op(self.dve, fn, r, w)

    def A(self, fn, r=(), w=()):
        return self.op(self.act, fn, r, w)

    def P(self, fn, r=(), w=()):
        return self.op(self.pe, fn, r, w)

    def G(self, fn, r=(), w=()):
        return self.op(self.pool, fn, r, w)

    def dma(self, q, out, in_, r=(), w=()):
        eng = {"sp": self.sp, "pool": self.pool, "act": self.act}[q]
        ring = self.rings[q]
        slot = ring[self.dcnt[q] % len(ring)]
        self.dcnt[q] += 1
        if slot[1] > 0:
            eng.wait((slot[0], slot[1]))
        self._pre(eng, r, w)
        ins = eng.raw.dma_start(out=out, in_=in_)
        slot[1] += 16
        ins.then_inc(slot[0], 16)
        tok = (slot[0], slot[1])
        self._post(tok, r, w)
        return tok

    def finish(self):
        for q in ("sp", "pool", "act"):
            for slot in self.rings[q]:
                if slot[1] > 0:
                    self.sp.wait((slot[0], slot[1]))
        for e in (self.pe, self.act, self.dve, self.pool):
            if e.cnt > 0:
                self.sp.wait((e.sem, e.cnt))


STOP = None


def build_program(T, L, TP):
    assert T % TP == 0 and TP % 512 == 0
    NPASS = T // TP
    NB = TP // 512
    NCH = TP // 64
    NTT = TP // 128
    nc = bass.Bass("TRN2", target_bir_lowering=False)
    mk = MK(nc)
    V, A, P, G, dma = mk.V, mk.A, mk.P, mk.G, mk.dma
    vec, act, pe, pool = nc.vector, nc.scalar, nc.tensor, nc.gpsimd

    def din(name, shape):
        return nc.dram_tensor(name, list(shape), F32, kind="ExternalInput").ap()

    def dout(name, shape):
        return nc.dram_tensor(name, list(shape), F32, kind="ExternalOutput").ap()

    xp = din("xp", [T, D])
    xs = din("xs", [NS, D])
    cT = din("cT", [128, KC, 17])
    sca = din("sca", [L, NS, 2, DC])
    scq = din("scq", [L, NS, 3, 3 * DC])
    ssm = din("ssm", [L, NS, NH, HD, HD])
    wall = din("wall", [L, NUNIT, 128, UNIT])
    wba_d = din("wba", [L, 128, KC * 16])
    ng_d = din("ng", [L, 128, KC])
    bada_d = din("bada", [L, 128, 48])
    cwa_d = din("cwa", [L, 128, 8 * 3])
    cwq_d = din("cwq", [L, 128, 24 * 4])
    alog_d = din("alog", [L, 64, 8])
    dtb_d = din("dtb", [L, 64, 8])
    ong_d = din("ong", [L, 128, 1])
    fng_d = din("fng", [128, D])

    y_p = dout("y_p", [T, D])
    y_s = dout("y_s", [NS, D])
    ca_p = dout("ca_p", [L, 2, DC])
    cq_p = dout("cq_p", [L, 3, 3 * DC])
    ss_p = dout("ss_p", [L, NH, HD, HD])
    ca_s = dout("ca_s", [L, NS, 2, DC])
    cq_s = dout("cq_s", [L, NS, 3, 3 * DC])
    ss_s = dout("ss_s", [L, NS, NH, HD, HD])

    xscr = nc.dram_tensor("xscr", [T + NS, D], F32, kind="Internal").ap()
    gscr = nc.dram_tensor("gscr", [L, 17, D], F32, kind="Internal").ap()
    GSCR = Tl(gscr)

    TW = TP + NS

    hT = mk.sb([128, KC, TW], BF16, "hT")
    catT = mk.sb([128, KC, TW], BF16, "catT")
    hT_b = [Tl(hT.h) for _ in range(NB + 1)]
    cat_b = [Tl(catT.h) for _ in range(NB + 1)]
    wring = [mk.sb([128, UNIT], BF16, f"wr{i}") for i in range(4)]
    wba = [mk.sb([128, KC * 16], BF16, f"wba{i}") for i in range(2)]
    ident_f = mk.sb([128, 128], F32, "ident_f")
    ident_b = mk.sb([128, 128], BF16, "ident_b")
    ones_b = mk.sb([128, 128], BF16, "ones_b")
    ones_f = mk.sb([128, 128], F32, "ones_f")
    Umat = mk.sb([64, 64], F32, "Umat")
    M1 = mk.sb([64, 64], F32, "M1")
    M2 = mk.sb([64, 64], F32, "M2")
    M3 = mk.sb([64, 64], F32, "M3")
    cT_f = mk.sb([128, KC, 17], F32, "cT_f")
    scT = mk.sb([128, KC, 17], BF16, "scT")
    mod_sb = mk.sb([128, 48, 17], F32, "mod_sb")
    ng = mk.sb([128, KC], F32, "ng")
    bada = mk.sb([128, 48], F32, "bada")
    cwa = mk.sb([128, 8, 3], F32, "cwa")
    cwq = mk.sb([128, 24, 4], F32, "cwq")
    alog = mk.sb([64, 8], F32, "alog")
    dtb = mk.sb([64, 8], F32, "dtb")
    ong = mk.sb([128, 1], F32, "ong")
    gsP = mk.sb([128, KC], F32, "gsP")
    gsS = mk.sb([128, KC, NS], F32, "gsS")
    Sst = mk.sb([128, NH, HD], F32, "Sst")
    Sst_b = [Tl(Sst.h) for _ in range(NH)]
    b_beta = mk.sb([64, NCH, 8], F32, "b_beta")
    b_lnb = mk.sb([64, NCH, 8], F32, "b_lnb")
    b_g = mk.sb([64, NCH, 8], F32, "b_g")
    b_gc = mk.sb([64, NCH, 8], F32, "b_gc")
    b_gcb = mk.sb([64, NCH, 8], F32, "b_gcb")
    b_tmp = mk.sb([64, NCH, 8], F32, "b_tmp")
    nexpA = mk.sb([64, 8], F32, "nexpA")
    s_beta = mk.sb([NS, 8], F32, "s_beta")
    s_a = mk.sb([NS, 8], F32, "s_a")
    s_tmp = mk.sb([NS, 16], F32, "s_tmp")
    s_dg = mk.sb([NS, 8, NS], F32, "s_dg")
    sb_bc = mk.sb([128, 8, NS], F32, "sb_bc")
    sa_bc = mk.sb([128, 8, NS], F32, "sa_bc")
    histA = mk.sb([128, 8, 2], F32, "histA")
    histQ = mk.sb([128, 24, 3], F32, "histQ")
    tailA = mk.sb([128, 2, 8], F32, "tailA")
    tailQ = mk.sb([128, 3, 24], F32, "tailQ")
    histA_b = [Tl(histA.h) for _ in range(8)]
    histQ_b = [Tl(histQ.h) for _ in range(24)]

    Nij = mk.sb([64, 8, 64], F32, "Nij"); Nji = mk.sb([64, 8, 64], F32, "Nji")
    Pa = mk.sb([64, 8, 64], F32, "Pa"); PTa = mk.sb([64, 8, 64], F32, "PTa")
    attnT2 = [mk.sb([64, 8, 64], F32, f"attnT{i}") for i in range(2)]
    Xt2 = [mk.sb([64, 8, 64], F32, f"Xt{i}") for i in range(2)]
    KBE = mk.sb([64, 8, 128], F32, "KBE")
    KTl2 = [mk.sb([64, 8, 128], F32, f"KTl{i}") for i in range(2)]
    VB2 = [mk.sb([64, 8, 128], F32, f"VB{i}") for i in range(2)]
    zsamp = mk.sb([128, 64, NS], F32, "zsamp")
    zsamp_b = [Tl(zsamp.h) for _ in range(16)]
    gstate = {"n": 0}
    u_sb = [mk.sb([64, 128], F32, f"u_sb{i}") for i in range(2)]
    IP = [mk.ps([128, 512], F32, f"ip{i}") for i in range(4)]
    GB = [mk.ps([128, 512], F32, f"gb{i}") for i in range(4)]

    ARENA = (nc.sbuf_bytes_remaining - 1024) // 4
    print('arena words', ARENA)
    arena = nc.alloc_sbuf_tensor("arena", [128, ARENA], F32)
    ast = {"off": 0}

    def carve(shape, dt=F32, parts=None):
        n = int(np.prod(shape[1:]))
        words = n if dt == F32 else (n + 1) // 2
        o = ast["off"]
        assert o + words <= ARENA, ("arena overflow", o, words)
        ast["off"] = o + words
        ap = arena[0:shape[0], o:o + words]
        if dt != F32:
            ap = ap.bitcast(dt)[:, 0:n]
        if len(shape) == 3:
            ap = ap.rearrange("p (a b) -> p a b", b=shape[2])
        return Tl(ap)

    def barrier():
        engs = (mk.pe, mk.act, mk.dve, mk.pool, mk.sp)
        for e in engs:
            for o in engs:
                if o is not e and o.cnt > 0:
                    e.wait((o.sem, o.cnt))
            for slot in mk.rings["sp"] + mk.rings["act"]:
                if slot[1] > 0:
                    e.wait((slot[0], slot[1]))
        ast["off"] = 0

    EPSB = mk.sb([128, 1], F32, "EPSB")
    G(lambda: pool.memset(EPSB[:], EPS), w=[EPSB])
    XB = {}

    def xb(u, tile):
        if (u, tile) not in XB:
            XB[(u, tile)] = Tl(xscr)
        return XB[(u, tile)]

    G(lambda: pool.memset(ident_f[:], 1.0), w=[ident_f])
    G(lambda: pool.affine_select(out=ident_f[:], in_=ident_f[:], pattern=[[-1, 128]], compare_op=ALU.is_equal,
                                 fill=0.0, base=0, channel_multiplier=1), r=[ident_f], w=[ident_f])
    V(lambda: vec.tensor_copy(ident_b[:], ident_f[:]), r=[ident_f], w=[ident_b])
    G(lambda: pool.memset(ones_b[:], 1.0), w=[ones_b])
    G(lambda: pool.memset(ones_f[:], 1.0), w=[ones_f])
    G(lambda: pool.memset(Umat[:], 1.0), w=[Umat])
    G(lambda: pool.affine_select(out=Umat[:], in_=Umat[:], pattern=[[1, 64]], compare_op=ALU.is_ge,
                                 fill=0.0, base=0, channel_multiplier=-1), r=[Umat], w=[Umat])
    for Mx, pat, cm, cop in ((M1, 1, -1, ALU.is_ge), (M2, 1, -1, ALU.is_gt), (M3, -1, 1, ALU.is_gt)):
        G(lambda Mx=Mx: pool.memset(Mx[:], 0.0), w=[Mx])
        G(lambda Mx=Mx, pat=pat, cm=cm, cop=cop: pool.affine_select(
            out=Mx[:], in_=Mx[:], pattern=[[pat, 64]], compare_op=cop, fill=NEG, base=0, channel_multiplier=cm),
          r=[Mx], w=[Mx])
    dma("sp", cT_f[:], cT[:, :, :], w=[cT_f])
    A(lambda: act.activation(out=scT[:], in_=cT_f[:], func=AF.Silu), r=[cT_f], w=[scT])

    if STOP == 'c':
        mk.finish(); return nc
    wq = []
    for l in range(L):
        for u in range(24):
            wq.append((l, u))
        for ps_ in range(NPASS):
            for u in range(24, 64):
                wq.append((l, u))
    wstate = {"next": 0}

    def wload_next():
        i = wstate["next"]
        if i >= len(wq):
            return
        l, u = wq[i]
        slot = wring[i % 4]
        dma("pool", slot[:], wall[l, u, :, :], w=[slot])
        wstate["next"] = i + 1

    wcons = {"i": 0}

    def wslot():
        return wring[wcons["i"] % 4]

    def wdone():
        wcons["i"] += 1
        wload_next()

    for _ in range(4):
        wload_next()

    bc3 = lambda ap, shape: ap.to_broadcast(list(shape))
    RR = lambda ap: ap.bitcast(F32R)

    for l in range(L):
        last = (l == L - 1)
        wb = wba[l % 2]
        dma("pool", wb[:], wba_d[l, :, :], w=[wb])
        dma("sp", ng[:], ng_d[l, :, :], w=[ng])
        dma("sp", bada[:], bada_d[l, :, :], w=[bada])
        dma("sp", cwa[:], cwa_d[l, :, :].rearrange("p (j k) -> p j k", k=3), w=[cwa])
        dma("sp", cwq[:], cwq_d[l, :, :].rearrange("p (j k) -> p j k", k=4), w=[cwq])
        dma("sp", alog[:], alog_d[l, :, :], w=[alog])
        dma("sp", dtb[:], dtb_d[l, :, :], w=[dtb])
        dma("sp", ong[:], ong_d[l, :, :], w=[ong])
        dma("sp", ca_s[l, :, 0, :], sca[l, :, 1, :])
        dma("sp", cq_s[l, :, 0:2, :], scq[l, :, 1:3, :])

        for u in range(24):
            slot = wslot()
            for ct in range(2):
                t_ = u * 2 + ct
                bank = GB[t_ // 16]
                for kc in range(KC):
                    P(lambda bank=bank, t_=t_, ct=ct, kc=kc, slot=slot: pe.matmul(
                        bank[:, (t_ % 16) * 32:(t_ % 16) * 32 + 17],
                        slot[:, ct * 2048 + kc * 128: ct * 2048 + (kc + 1) * 128],
                        scT[:, kc, :], start=(kc == 0), stop=(kc == KC - 1)),
                      r=[slot, scT], w=[bank])
            wdone()
        for bi in range(3):
            V(lambda bi=bi: vec.tensor_tensor(
                mod_sb[:, bi * 16:(bi + 1) * 16, :],
                GB[bi][:, :].rearrange("p (a b) -> p a b", b=32)[:, :, 0:17],
                bc3(bada[:, bi * 16:(bi + 1) * 16, None], [128, 16, 17]), ALU.add),
              r=[GB[bi], bada], w=[mod_sb])
        V(lambda: vec.scalar_tensor_tensor(gsP[:], mod_sb[:, 16:32, 0], 1.0, ng[:], ALU.add, ALU.mult),
          r=[mod_sb, ng], w=[gsP])
        V(lambda: vec.scalar_tensor_tensor(gsS[:], mod_sb[:, 16:32, 1:17], 1.0,
                                           bc3(ng[:, :, None], [128, KC, NS]), ALU.add, ALU.mult),
          r=[mod_sb, ng], w=[gsS])
        barrier()
        gate_tok = carve([17, D])
        for kc in range(KC):
            bk = GB[kc // 4]
            P(lambda kc=kc, bk=bk: pe.transpose(bk[0:17, (kc % 4) * 128:(kc % 4 + 1) * 128],
                                                mod_sb[:, 32 + kc, :], ident_f[:]),
              r=[mod_sb, ident_f], w=[bk])
        for q4 in range(4):
            A(lambda q4=q4: act.copy(gate_tok[:, q4 * 512:(q4 + 1) * 512], GB[q4][0:17, :]),
              r=[GB[q4]], w=[gate_tok])
        dma("sp", gscr[l, :, :], gate_tok[:], r=[gate_tok], w=[GSCR])

        if STOP == 'mod':
            mk.finish(); return nc
        for ps_ in range(NPASS):
            t0 = ps_ * TP
            do_s = (ps_ == 0)
            ntile = NTT + (1 if do_s else 0)
            barrier()
            xt = [carve([128, D]) for _ in range(3)]
            xn = [carve([128, D], BF16) for _ in range(3)]
            stat = [carve([128, 4]) for _ in range(3)]
            tmpS = carve([128, KC, NS])
            for tt in range(ntile):
                is_s = (tt == NTT)
                np_ = NS if is_s else 128
                xti, xni, sti = xt[tt % 3], xn[tt % 3], stat[tt % 3]
                gt_ = (T // 128) if is_s else (t0 // 128 + tt)
                if is_s:
                    sap = xs[:, :] if l == 0 else xscr[T:T + NS, :]
                else:
                    sap = (xp if l == 0 else xscr)[t0 + tt * 128: t0 + (tt + 1) * 128, :]
                dma("sp", xti[0:np_, :], sap, r=([] if l == 0 else [xb(u_, gt_) for u_ in range(8)]), w=[xti])
                A(lambda: act.activation(out=xni[0:np_, :], in_=xti[0:np_, :], func=AF.Square,
                                         accum_out=sti[0:np_, 0:1]), r=[xti], w=[xni, sti])
                V(lambda: vec.tensor_scalar(sti[0:np_, 1:2], sti[0:np_, 0:1], 1.0 / D, EPS, ALU.mult, ALU.add),
                  r=[sti], w=[sti])
                A(lambda: act.activation(out=sti[0:np_, 2:3], in_=sti[0:np_, 1:2], func=AF.Ln), r=[sti], w=[sti])
                A(lambda: act.activation(out=sti[0:np_, 3:4], in_=sti[0:np_, 2:3], func=AF.Exp, scale=-0.5),
                  r=[sti], w=[sti])
                A(lambda: act.activation(out=xni[0:np_, :], in_=xti[0:np_, :], func=AF.Copy, scale=sti[0:np_, 3:4]),
                  r=[xti, sti], w=[xni])
                pb = (GB[0], GB[1]) if tt % 2 == 0 else (GB[2], GB[3])
                for kc in range(KC):
                    bk = pb[kc // 8]
                    P(lambda kc=kc, bk=bk: pe.transpose(
                        bk[:, :].bitcast(BF16)[:, (kc % 8) * 128:(kc % 8) * 128 + np_],
                        xni[0:np_, kc * 128:(kc + 1) * 128], ident_b[0:np_, 0:np_]),
                      r=[xni, ident_b], w=[bk])
                if not is_s:
                    blk = tt // 4
                    c0 = tt * 128
                    for kc in range(KC):
                        bk = pb[kc // 8]
                        src_ap = lambda kc=kc, bk=bk: bk[:, :].bitcast(BF16)[:, (kc % 8) * 128:(kc % 8 + 1) * 128]
                        if kc < 8:
                            A(lambda kc=kc, s=src_ap: act.activation(
                                out=hT[:, kc, c0:c0 + 128], in_=s(), func=AF.Identity,
                                scale=gsP[:, kc:kc + 1], bias=mod_sb[:, kc, 0:1]),
                              r=[bk, gsP, mod_sb], w=[hT_b[blk]])
                        else:
                            V(lambda kc=kc, s=src_ap: vec.tensor_scalar(
                                hT[:, kc, c0:c0 + 128], s(), gsP[:, kc:kc + 1], mod_sb[:, kc, 0:1],
                                ALU.mult, ALU.add),
                              r=[bk, gsP, mod_sb], w=[hT_b[blk]])
                else:
                    for half in range(2):
                        bk = pb[half]
                        V(lambda half=half, bk=bk: vec.tensor_tensor(
                            tmpS[:, half * 8:(half + 1) * 8, :],
                            bk[:, :].bitcast(BF16).rearrange("p (a b) -> p a b", b=128)[:, 0:8, 0:NS],
                            gsS[:, half * 8:(half + 1) * 8, :], ALU.mult),
                          r=[bk, gsS], w=[tmpS])
                    V(lambda: vec.tensor_tensor(hT[:, :, TP:TP + NS], tmpS[:], mod_sb[:, 0:16, 1:17], ALU.add),
                      r=[tmpS, mod_sb], w=[hT_b[NB]])

            if STOP == 'A':
                mk.finish(); return nc
            pba = GB[0]
            for n in range(NCH):
                for kc in range(KC):
                    P(lambda n=n, kc=kc: pe.matmul(pba[0:64, n * 16:(n + 1) * 16], hT[:, kc, n * 64:(n + 1) * 64],
                                                   wb[:, kc * 16:(kc + 1) * 16], start=(kc == 0), stop=(kc == KC - 1)),
                      r=[hT_b[n // 8], wb], w=[pba])
            pv = lambda sl: pba[0:64, 0:NCH * 16].rearrange("p (n c) -> p n c", c=16)[:, :, sl]
            W_ = [64, NCH, 8]
            A(lambda: act.activation(out=nexpA[:], in_=alog[:], func=AF.Exp), r=[alog], w=[nexpA])
            V(lambda: vec.tensor_scalar(nexpA[:], nexpA[:], -1.0, None, ALU.mult), r=[nexpA], w=[nexpA])
            A(lambda: act.activation(out=b_tmp[:], in_=pv(slice(0, 8)), func=AF.Exp, scale=-1.0), r=[pba], w=[b_tmp])
            A(lambda: act.activation(out=b_lnb[:], in_=b_tmp[:], func=AF.Ln, bias=1.0), r=[b_tmp], w=[b_lnb])
            A(lambda: act.activation(out=b_beta[:], in_=b_lnb[:], func=AF.Exp, scale=-1.0), r=[b_lnb], w=[b_beta])
            V(lambda: vec.tensor_tensor(b_tmp[:], pv(slice(8, 16)), bc3(dtb[:, None, :], W_), ALU.add),
              r=[pba, dtb, b_lnb], w=[b_tmp])
            A(lambda: act.activation(out=b_tmp[:], in_=b_tmp[:], func=AF.Exp), r=[b_tmp], w=[b_tmp])
            A(lambda: act.activation(out=b_tmp[:], in_=b_tmp[:], func=AF.Ln, bias=1.0), r=[b_tmp], w=[b_tmp])
            V(lambda: vec.tensor_tensor(b_g[:], b_tmp[:], bc3(nexpA[:, None, :], W_), ALU.mult),
              r=[b_tmp, nexpA], w=[b_g])
            pgc = GB[1]
            P(lambda: pe.matmul(pgc[0:64, 0:NCH * 8], Umat[:], b_g[:].rearrange("p n c -> p (n c)"),
                                start=True, stop=True), r=[Umat, b_g], w=[pgc])
            A(lambda: act.copy(b_gc[:].rearrange("p n c -> p (n c)"), pgc[0:64, 0:NCH * 8]), r=[pgc], w=[b_gc])
            V(lambda: vec.tensor_tensor(b_gcb[:], b_gc[:], b_lnb[:], ALU.subtract), r=[b_gc, b_lnb], w=[b_gcb])
            if do_s:
                psb = GB[2]
                for kc in range(KC):
                    P(lambda kc=kc: pe.matmul(psb[0:NS, 0:16], hT[:, kc, TP:TP + NS], wb[:, kc * 16:(kc + 1) * 16],
                                              start=(kc == 0), stop=(kc == KC - 1)), r=[hT_b[NB], wb], w=[psb])
                A(lambda: act.activation(out=s_tmp[:, 0:8], in_=psb[0:NS, 0:8], func=AF.Exp, scale=-1.0),
                  r=[psb], w=[s_tmp])
                A(lambda: act.activation(out=s_tmp[:, 0:8], in_=s_tmp[:, 0:8], func=AF.Ln, bias=1.0),
                  r=[s_tmp], w=[s_tmp])
                A(lambda: act.activation(out=s_beta[:], in_=s_tmp[:, 0:8], func=AF.Exp, scale=-1.0),
                  r=[s_tmp], w=[s_beta])
                V(lambda: vec.tensor_tensor(s_tmp[:, 8:16], psb[0:NS, 8:16], dtb[0:NS, :], ALU.add),
                  r=[psb, dtb], w=[s_tmp])
                A(lambda: act.activation(out=s_tmp[:, 8:16], in_=s_tmp[:, 8:16], func=AF.Exp), r=[s_tmp], w=[s_tmp])
                A(lambda: act.activation(out=s_tmp[:, 8:16], in_=s_tmp[:, 8:16], func=AF.Ln, bias=1.0),
                  r=[s_tmp], w=[s_tmp])
                V(lambda: vec.tensor_tensor(s_tmp[:, 8:16], s_tmp[:, 8:16], nexpA[0:NS, :], ALU.mult),
                  r=[s_tmp, nexpA], w=[s_tmp])
                A(lambda: act.activation(out=s_a[:], in_=s_tmp[:, 8:16], func=AF.Exp), r=[s_tmp], w=[s_a])
                for srcv, dst in ((s_beta, sb_bc), (s_a, sa_bc)):
                    V(lambda srcv=srcv: vec.tensor_tensor(
                        s_dg[:], bc3(ident_f[0:NS, None, 0:NS], [NS, 8, NS]), bc3(srcv[:, :, None], [NS, 8, NS]),
                        ALU.mult), r=[ident_f, srcv], w=[s_dg])
                    P(lambda: pe.matmul(GB[3][:, 0:8 * NS], ones_f[0:NS, :], s_dg[:].rearrange("p a b -> p (a b)"),
                                        start=True, stop=True), r=[ones_f, s_dg], w=[GB[3]])
                    A(lambda dst=dst: act.copy(dst[:].rearrange("p a b -> p (a b)"), GB[3][:, 0:8 * NS]),
                      r=[GB[3]], w=[dst])

            if STOP == 'BA':
                mk.finish(); return nc
            barrier()
            acc = [carve([128, 512]) for _ in range(3)]
            ca_sb, sg = acc[1], acc[2]
            pq = [carve([128, 3 + 512]) for _ in range(3)]
            rn = carve([128, 512]); sqb = carve([128, 512], BF16)
            qn = carve([128, 512], BF16); kn = carve([128, 512], BF16); vsb = carve([128, 512], BF16)
            sgate2 = [carve([128, 512]) for _ in range(2)]
            qdec2 = [carve([128, 512]) for _ in range(2)]
            wdn2 = [carve([128, 512]) for _ in range(2)]
            eg2 = [carve([128, 512]) for _ in range(2)]
            dg1 = carve([64, 8, 64]); dg2 = carve([64, 8, 64])
            a1, a2 = dg1, dg2
            a3 = carve([64, 8, 64])
            sc1 = carve([64, 8]); sc2 = carve([64, 8])
            ot = carve([128, 512]); rn2 = carve([128, 512])
            osq = Tl(rn2.h.bitcast(BF16)[:, 0:512], rn2.b)

            def inproj(slot, banks, blk):
                c0, hb = blk * 512, hT_b[blk]
                for ct in range(2):
                    for kc in range(KC):
                        P(lambda ct=ct, kc=kc: pe.matmul(
                            banks[ct][:, 0:512], slot[:, ct * 2048 + kc * 128: ct * 2048 + (kc + 1) * 128],
                            hT[:, kc, c0:c0 + 512], start=(kc == 0), stop=(kc == KC - 1)),
                          r=[slot, hb], w=[banks[ct]])

            def sample_inproj(sa, sb_, g):
                for slot_, base in ((sa, 0), (sb_, 2)):
                    for ct in range(2):
                        for kc in range(KC):
                            P(lambda ct=ct, kc=kc, slot_=slot_, base=base: pe.matmul(
                                GB[2][:, (base + ct) * NS:(base + ct + 1) * NS],
                                slot_[:, ct * 2048 + kc * 128: ct * 2048 + (kc + 1) * 128],
                                hT[:, kc, TP:TP + NS], start=(kc == 0), stop=(kc == KC - 1)),
                              r=[slot_, hT_b[NB]], w=[GB[2]])
                A(lambda: act.copy(zsamp[:, g * 4:(g + 1) * 4, :].rearrange("p a b -> p (a b)"), GB[2][:, 0:4 * NS]),
                  r=[GB[2]], w=[zsamp_b[g]])

            pend = {"gen": None}
            pend2 = {"gen": None}

            def inproj_gen(sa, sb_, blk):
                hb = hT_b[blk]
                c0 = blk * 512
                for slot_, banks in ((sa, (IP[0], IP[1])), (sb_, (IP[2], IP[3]))):
                    for ct in range(2):
                        for kc in range(KC):
                            P(lambda ct=ct, kc=kc, slot_=slot_, banks=banks: pe.matmul(
                                banks[ct][:, 0:512], slot_[:, ct * 2048 + kc * 128: ct * 2048 + (kc + 1) * 128],
                                hT[:, kc, c0:c0 + 512], start=(kc == 0), stop=(kc == KC - 1)),
                              r=[slot_, hb], w=[banks[ct]])
                            if kc % 2 == 1:
                                yield

            def _pump(pd, n):
                for _ in range(n):
                    if pd["gen"] is None:
                        return
                    try:
                        next(pd["gen"])
                    except StopIteration:
                        pd["gen"] = None
                        return

            def pump(n=1):
                _pump(pend, n)

            def pump2(n=1):
                _pump(pend2, n)

            def drain():
                while pend["gen"] is not None:
                    pump()

            def drain2():
                while pend2["gen"] is not None:
                    pump2()

            def rec_gen(hd, blk, par, lastblk):
                Sb, Sap = Sst_b[hd], Sst[:, hd, :]
                X_, VB_, KT_, AT_ = Xt2[par], VB2[par], KTl2[par], attnT2[par]
                wdn_, qd_, eg_, sgt_ = wdn2[par], qdec2[par], eg2[par], sgate2[par]
                B3 = GB[3]
                for c in range(8):
                    cs = slice(c * 64, (c + 1) * 64)
                    us = u_sb[c % 2]
                    pu = B3[0:64, 128:256]
                    po = B3[:, (c % 2) * 64:(c % 2) * 64 + 64]
                    pS = B3[:, 256:384]
                    P(lambda: pe.matmul(pu, RR(X_[:, c, :]), RR(VB_[:, c, :]), start=True, stop=False),
                      r=[X_, VB_], w=[B3])
                    P(lambda: pe.matmul(pu, wdn_[:, cs], Sap, start=False, stop=True), r=[wdn_, Sb], w=[B3])
                    A(lambda: act.copy(RR(us[:]), pu), r=[B3], w=[us])
                    yield
                    P(lambda: pe.matmul(po, Sap, qd_[:, cs], start=True, stop=False), r=[Sb, qd_], w=[B3])
                    P(lambda: pe.matmul(po, RR(us[:]), RR(AT_[:, c, :]), start=False, stop=True), r=[us, AT_], w=[B3])
                    P(lambda: pe.matmul(pS, RR(KT_[:, c, :]), RR(us[:]), start=True, stop=True), r=[KT_, us], w=[B3])
                    A(lambda: act.copy(ot[:, cs], po), r=[B3], w=[ot])
                    V(lambda: vec.scalar_tensor_tensor(Sap, Sap, eg_[:, c * 64 + 63: c * 64 + 64], pS, ALU.mult, ALU.add),
                      r=[Sb, eg_, B3], w=[Sb])
                    yield
                A(lambda: act.activation(out=osq[:], in_=ot[:], func=AF.Square), r=[ot], w=[osq])
                P(lambda: pe.matmul(B3[:], ones_b[:], osq[:], start=True, stop=True), r=[ones_b, osq], w=[B3])
                A(lambda: act.activation(out=rn2[:], in_=B3[:], func=AF.Ln, scale=1.0 / HD, bias=EPSB[:, 0:1]),
                  r=[B3, EPSB], w=[rn2])
                A(lambda: act.activation(out=rn2[:], in_=rn2[:], func=AF.Exp, scale=-0.5), r=[rn2], w=[rn2])
                yield
                V(lambda: vec.tensor_tensor(ot[:], ot[:], rn2[:], ALU.mult), r=[ot, rn2], w=[ot])
                V(lambda: vec.scalar_tensor_tensor(catT[:, 8 + hd, blk * 512:(blk + 1) * 512], ot[:],
                                                   ong[:, 0:1], sgt_[:], ALU.mult, ALU.mult),
                  r=[ot, ong, sgt_], w=[cat_b[blk]])
                if lastblk and ps_ == NPASS - 1:
                    dma("sp", ss_p[l, hd, :, :], Sap, r=[Sb])

            for j in range(8):
                s1 = wslot()
                s2 = wring[(wcons["i"] + 1) % 4]
                ch = Tl(pq[0].h[:, 0:514], pq[0].b)
                if ps_ == 0:
                    V(lambda: vec.memset(ch[:, 0:2], 0.0), w=[ch])
                else:
                    V(lambda j=j: vec.tensor_copy(ch[:, 0:2], histA[:, j, :]), r=[histA_b[j]], w=[ch])
                if do_s:
                    sample_inproj(s1, s2, j)
                for blk in range(NB):
                    inproj(s1, (IP[0], IP[1]), blk)
                    inproj(s2, (IP[2], IP[3]), blk)
                    A(lambda: act.copy(ca_sb[:], IP[0][:]), r=[IP[0]], w=[ca_sb])
                    V(lambda: vec.tensor_tensor(ch[:, 2:514], ca_sb[:], IP[1][:], ALU.mult), r=[ca_sb, IP[1]], w=[ch])
                    V(lambda j=j: vec.tensor_scalar(acc[0][:], ch[:, 2:514], cwa[:, j, 2:3], None, ALU.mult),
                      r=[ch, cwa], w=[acc[0]])
                    for tap in (1, 0):
                        V(lambda j=j, tap=tap: vec.scalar_tensor_tensor(
                            acc[0][:], ch[:, tap:tap + 512], cwa[:, j, tap:tap + 1], acc[0][:], ALU.mult, ALU.add),
                          r=[ch, cwa, acc[0]], w=[acc[0]])
                    A(lambda: act.activation(out=sg[:], in_=IP[3][:], func=AF.Silu), r=[IP[3]], w=[sg])
                    V(lambda: vec.tensor_tensor(acc[0][:], acc[0][:], IP[2][:], ALU.mult), r=[acc[0], IP[2]], w=[acc[0]])
                    V(lambda j=j, blk=blk: vec.tensor_tensor(catT[:, j, blk * 512:(blk + 1) * 512], acc[0][:], sg[:],
                                                             ALU.mult), r=[acc[0], sg], w=[cat_b[blk]])
                    if blk == NB - 1:
                        A(lambda j=j: act.copy(histA[:, j, :], ch[:, 512:514]), r=[ch], w=[histA_b[j]])
                        if ps_ == NPASS - 1:
                            A(lambda j=j: act.copy(tailA[:, :, j], ch[:, 512:514]), r=[ch], w=[tailA])
                    else:
                        A(lambda: act.copy(ch[:, 0:2], ch[:, 512:514]), r=[ch], w=[ch])
                wdone()
                wdone()

            for hd in range(NH):
                s1 = wslot()
                s2 = wring[(wcons["i"] + 1) % 4]
                Sb = Sst_b[hd]
                Sap = Sst[:, hd, :]
                if ps_ == 0:
                    V(lambda: vec.memset(Sap, 0.0), w=[Sb])
                hq = [histQ_b[tn * 8 + hd] for tn in range(3)]
                for tn in range(3):
                    if ps_ == 0:
                        V(lambda tn=tn: vec.memset(pq[tn][:, 0:3], 0.0), w=[pq[tn]])
                    else:
                        V(lambda tn=tn: vec.tensor_copy(pq[tn][:, 0:3], histQ[:, tn * 8 + hd, :]),
                          r=[hq[tn]], w=[pq[tn]])
                if do_s:
                    sample_inproj(s1, s2, 8 + hd)
                cw = lambda tn, tap: cwq[:, tn * 8 + hd, tap:tap + 1]
                for blk in range(NB):
                    par = gstate["n"] % 2
                    gstate["n"] += 1
                    Xt, VB, KTl, attnT = Xt2[par], VB2[par], KTl2[par], attnT2[par]
                    wdn, qdec, eg128, sgate = wdn2[par], qdec2[par], eg2[par], sgate2[par]
                    if pend["gen"] is None:
                        pend["gen"] = inproj_gen(s1, s2, blk)
                    drain()
                    if blk == NB - 1:
                        wdone()
                        wdone()
                    ch0 = blk * 8
                    sv = lambda t_: t_[:, ch0:ch0 + 8, hd]
                    for tn in range(3):
                        A(lambda tn=tn: act.copy(pq[tn][:, 3:515], IP[tn][:]), r=[IP[tn]], w=[pq[tn]])
                    A(lambda: act.activation(out=sgate[:], in_=IP[3][:], func=AF.Silu), r=[IP[3]], w=[sgate])
                    if blk < NB - 1:
                        pend["gen"] = inproj_gen(s1, s2, blk + 1)
                    elif hd < NH - 1:
                        pend["gen"] = inproj_gen(wslot(), wring[(wcons["i"] + 1) % 4], 0)
                    def conv_t(tn):
                        V(lambda: vec.tensor_scalar(acc[tn][:], pq[tn][:, 3:515], cw(tn, 3), None, ALU.mult),
                          r=[pq[tn], cwq], w=[acc[tn]])
                        for tap in (2, 1, 0):
                            V(lambda tap=tap: vec.scalar_tensor_tensor(
                                acc[tn][:], pq[tn][:, tap:tap + 512], cw(tn, tap), acc[tn][:], ALU.mult, ALU.add),
                              r=[pq[tn], cwq, acc[tn]], w=[acc[tn]])
                        if blk == NB - 1:
                            A(lambda: act.copy(histQ[:, tn * 8 + hd, :], pq[tn][:, 512:515]), r=[pq[tn]], w=[hq[tn]])
                            if ps_ == NPASS - 1:
                                A(lambda: act.copy(tailQ[:, :, tn * 8 + hd], pq[tn][:, 512:515]), r=[pq[tn]], w=[tailQ])
                        else:
                            A(lambda: act.copy(pq[tn][:, 0:3], pq[tn][:, 512:515]), r=[pq[tn]], w=[pq[tn]])

                    def silu_t(tn):
                        if tn < 2:
                            A(lambda: act.activation(out=acc[tn][:], in_=acc[tn][:], func=AF.Silu), r=[acc[tn]], w=[acc[tn]])
                        else:
                            A(lambda: act.activation(out=vsb[:], in_=acc[2][:], func=AF.Silu), r=[acc[2]], w=[vsb])

                    def l2norm_t(tn, dst, scl, bank):
                        A(lambda: act.activation(out=sqb[:], in_=acc[tn][:], func=AF.Square), r=[acc[tn]], w=[sqb])
                        P(lambda: pe.matmul(bank[:], ones_b[:], sqb[:], start=True, stop=True), r=[ones_b, sqb], w=[bank])
                        A(lambda: act.activation(out=rn[:], in_=bank[:], func=AF.Ln, bias=EPSB[:, 0:1]),
                          r=[bank, EPSB], w=[rn])
                        A(lambda: act.activation(out=rn[:], in_=rn[:], func=AF.Exp, scale=-0.5), r=[rn], w=[rn])
                        V(lambda: vec.scalar_tensor_tensor(dst[:], acc[tn][:], scl, rn[:], ALU.mult, ALU.mult),
                          r=[acc[tn], rn], w=[dst])

                    idb = bc3(ident_f[0:64, None, 0:64], [64, 8, 64])
                    V(lambda: vec.tensor_tensor(dg1[:], idb, bc3(sv(b_gc)[:, :, None], [64, 8, 64]), ALU.mult),
                      r=[ident_f, b_gc], w=[dg1])
                    V(lambda: vec.tensor_tensor(dg2[:], idb, bc3(sv(b_gcb)[:, :, None], [64, 8, 64]), ALU.mult),
                      r=[ident_f, b_gcb], w=[dg2])
                    fl = lambda t_: t_[:].rearrange("p a b -> p (a b)")
                    P(lambda: pe.matmul(GB[0][:], ones_f[0:64, :], fl(dg1), start=True, stop=True),
                      r=[ones_f, dg1], w=[GB[0]])
                    P(lambda: pe.matmul(GB[1][0:64, :], ones_f[0:64, 0:64], fl(dg2), start=True, stop=True),
                      r=[ones_f, dg2], w=[GB[1]])
                    pump(3); pump2()
                    conv_t(1)
                    silu_t(1)
                    pump(3); pump2()
                    R3 = lambda bk: bk[0:64, :].rearrange("p (a b) -> p a b", b=64)
                    gcb3 = bc3(sv(b_gc)[:, :, None], [64, 8, 64])
                    gcbb3 = bc3(sv(b_gcb)[:, :, None], [64, 8, 64])
                    msk = lambda M: bc3(M[:, None, :], [64, 8, 64])
                    V(lambda: vec.tensor_tensor(a1[:], R3(GB[0]), gcb3, ALU.subtract), r=[GB[0], b_gc], w=[a1])
                    V(lambda: vec.tensor_tensor(a1[:], a1[:], msk(M1), ALU.add), r=[a1, M1], w=[a1])
                    V(lambda: vec.tensor_tensor(a2[:], R3(GB[1]), gcb3, ALU.subtract), r=[GB[1], b_gc], w=[a2])
                    V(lambda: vec.tensor_tensor(a2[:], a2[:], msk(M2), ALU.add), r=[a2, M2], w=[a2])
                    V(lambda: vec.tensor_tensor(a3[:], gcbb3, R3(GB[0]), ALU.subtract), r=[GB[0], b_gcb], w=[a3])
                    V(lambda: vec.tensor_tensor(a3[:], a3[:], msk(M3), ALU.add), r=[a3, M3], w=[a3])
                    V(lambda: vec.tensor_tensor(sc2[:], R3(GB[0])[:, :, 63], sv(b_gc), ALU.subtract),
                      r=[GB[0], b_gc], w=[sc2])
                    A(lambda: act.activation(out=eg128[:], in_=GB[0][:], func=AF.Exp), r=[GB[0]], w=[eg128])
                    l2norm_t(1, kn, 1.0, GB[2])
                    pump(3); pump2()
                    deferred = [lambda: conv_t(0), lambda: conv_t(2),
                                lambda: (silu_t(2), silu_t(0)), lambda: l2norm_t(0, qn, HD ** -0.5, GB[1])]
                    for a_ in (a2, a3, a1):
                        A(lambda a_=a_: act.activation(out=a_[:], in_=a_[:], func=AF.Exp), r=[a_], w=[a_])
                    A(lambda: act.activation(out=sc1[:], in_=sv(b_gcb), func=AF.Exp), r=[b_gcb], w=[sc1])
                    A(lambda: act.activation(out=sc2[:], in_=sc2[:], func=AF.Exp), r=[sc2], w=[sc2])
                    pump(3); pump2()
                    for c in range(8):
                        cs = slice(c * 64, (c + 1) * 64)
                        P(lambda cs=cs: pe.matmul(GB[2][0:64, cs], kn[:, cs], kn[:, cs], start=True, stop=True),
                          r=[kn], w=[GB[2]])
                    pump(3); pump2()
                    V(lambda: vec.scalar_tensor_tensor(RR(Nij[:]), R3(GB[2]), -1.0, a3[:], ALU.mult, ALU.mult),
                      r=[GB[2], a3], w=[Nij])
                    V(lambda: vec.scalar_tensor_tensor(RR(Nji[:]), R3(GB[2]), -1.0, a2[:], ALU.mult, ALU.mult),
                      r=[GB[2], a2], w=[Nji])
                    V(lambda: vec.tensor_tensor(RR(Xt[:]), Nji[:], idb, ALU.add), r=[Nji, ident_f], w=[Xt])
                    Pc, PTc = Nij, Nji

                    def xupd(Pn_):
                        for c in range(8):
                            cs = slice(c * 64, (c + 1) * 64)
                            P(lambda c=c, cs=cs: pe.matmul(GB[2][0:64, cs], RR(Pn_[:, c, :]), RR(Xt[:, c, :]),
                                                           start=True, stop=True), r=[Pn_, Xt], w=[GB[2]])

                    def xadd():
                        V(lambda: vec.tensor_tensor(RR(fl(Xt)), fl(Xt), GB[2][0:64, :], ALU.add), r=[Xt, GB[2]], w=[Xt])

                    for lev in range(5):
                        lastlev = (lev == 4)
                        for c in range(8):
                            cs = slice(c * 64, (c + 1) * 64)
                            P(lambda c=c, cs=cs, Pc=Pc, PTc=PTc: pe.matmul(GB[0][0:64, cs], RR(PTc[:, c, :]), RR(Pc[:, c, :]),
                                                                           start=True, stop=True),
                              r=[Pc, PTc], w=[GB[0]])
                            if not lastlev:
                                P(lambda c=c, cs=cs, Pc=Pc, PTc=PTc: pe.matmul(GB[1][0:64, cs], RR(Pc[:, c, :]),
                                                                               RR(PTc[:, c, :]), start=True, stop=True),
                                  r=[Pc, PTc], w=[GB[1]])
                        if lev >= 1:
                            xupd(Pc)
                        pump2()
                        Pn, PTn = (Pa, PTa) if lev % 2 == 0 else (Nij, Nji)
                        A(lambda Pn=Pn: act.copy(RR(fl(Pn)), GB[0][0:64, :]), r=[GB[0]], w=[Pn])
                        if not lastlev:
                            V(lambda PTn=PTn: vec.tensor_copy(RR(fl(PTn)), GB[1][0:64, :]), r=[GB[1]], w=[PTn])
                        if lev >= 1:
                            xadd()
                        pump(); pump2()
                        Pc, PTc = Pn, PTn
                        if lev < len(deferred):
                            deferred[lev]()
                    xupd(Pc)
                    pump2()
                    xadd()
                    for c in range(8):
                        cs = slice(c * 64, (c + 1) * 64)
                        P(lambda cs=cs: pe.matmul(GB[2][0:64, cs], kn[:, cs], qn[:, cs], start=True, stop=True),
                          r=[kn, qn], w=[GB[2]])
                    V(lambda: vec.tensor_tensor(RR(attnT[:]), R3(GB[2]), a1[:], ALU.mult), r=[GB[2], a1], w=[attnT])
                    V(lambda: vec.tensor_tensor(qdec[:], qn[:], eg128[:], ALU.mult), r=[qn, eg128], w=[qdec])
                    for c in range(8):
                        cs = slice(c * 64, (c + 1) * 64)
                        P(lambda c=c, cs=cs: pe.transpose(GB[0][0:64, :].bitcast(BF16)[:, c * 128:(c + 1) * 128],
                                                          kn[:, cs], ident_b[:]), r=[kn, ident_b], w=[GB[0]])
                        P(lambda c=c, cs=cs: pe.transpose(GB[1][0:64, :].bitcast(BF16)[:, c * 128:(c + 1) * 128],
                                                          vsb[:, cs], ident_b[:]), r=[vsb, ident_b], w=[GB[1]])
                    pump(2); pump2()
                    T3 = lambda bk: bk[0:64, :].bitcast(BF16).rearrange("p (a b) -> p a b", b=128)
                    s3 = lambda t_: bc3(t_[:, :, None], [64, 8, 128])
                    V(lambda: vec.tensor_tensor(RR(KBE[:]), T3(GB[0]), s3(sc1), ALU.mult), r=[GB[0], sc1], w=[KBE])
                    V(lambda: vec.tensor_tensor(RR(KTl[:]), T3(GB[0]), s3(sc2), ALU.mult), r=[GB[0], sc2], w=[KTl])
                    V(lambda: vec.tensor_tensor(RR(VB[:]), T3(GB[1]), bc3(sv(b_beta)[:, :, None], [64, 8, 128]), ALU.mult),
                      r=[GB[1], b_beta], w=[VB])
                    for c in range(8):
                        cs = slice(c * 64, (c + 1) * 64)
                        P(lambda c=c, cs=cs: pe.matmul(GB[2][:, cs], RR(KBE[:, c, :]), RR(Xt[:, c, :]), start=True, stop=True),
                          r=[KBE, Xt], w=[GB[2]])
                    pump(); pump2()
                    A(lambda: act.mul(wdn[:], GB[2][:], -1.0), r=[GB[2]], w=[wdn])
                    drain2()
                    pend2["gen"] = rec_gen(hd, blk, par, blk == NB - 1)
            drain()
            drain2()

            if do_s:
                barrier()
                ha_tok = carve([NS, 2, 128]); hist_as = carve([128, 2, NS])
                zc = carve([128, NS]); acc_c = carve([128, NS]); sg_c = carve([128, NS])
                strow_c = [carve([NS, 128]) for _ in range(2)]

                def head_scratch():
                    return (carve([NS, 9, 128]), carve([128, 9, NS]), [carve([128, NS]) for _ in range(3)],
                            carve([128, NS, 2]), carve([128, NS]), [carve([128, NS]) for _ in range(4)],
                            carve([128, NS]), carve([128, NS, 128]), carve([NS, 128]), carve([NS, 128]),
                            carve([NS, 8, 128]), [carve([NS, 128]) for _ in range(2)])

                def conv_gen():
                    strow_i = [0]
                    strow = strow_c
                    sg_s = sg_c
                    accs = [acc_c]
                    for j in range(8):
                        zb = zsamp_b[j]
                        zz = lambda i: zsamp[:, j * 4 + i, :]
                        dma("sp", ha_tok[:], sca[l, :, :, j * 128:(j + 1) * 128], w=[ha_tok])
                        for r_ in range(2):
                            yield
                            P(lambda r_=r_: pe.transpose(GB[3][:, r_ * NS:(r_ + 1) * NS], ha_tok[:, r_, :],
                                                         ident_f[0:NS, 0:NS]), r=[ha_tok, ident_f], w=[GB[3]])
                        A(lambda: act.copy(hist_as[:].rearrange("p a b -> p (a b)"), GB[3][:, 0:2 * NS]),
                          r=[GB[3]], w=[hist_as])
                        V(lambda: vec.tensor_tensor(zc[:], zz(0), zz(1), ALU.mult), r=[zb], w=[zc])
                        V(lambda j=j: vec.tensor_scalar(accs[0][:], zc[:], cwa[:, j, 2:3], None, ALU.mult),
                          r=[zc, cwa], w=[accs[0]])
                        for tap in (1, 0):
                            V(lambda j=j, tap=tap: vec.scalar_tensor_tensor(
                                accs[0][:], hist_as[:, tap, :], cwa[:, j, tap:tap + 1], accs[0][:],
                                ALU.mult, ALU.add), r=[hist_as, cwa, accs[0]], w=[accs[0]])
                        A(lambda: act.activation(out=sg_s[:], in_=zz(3), func=AF.Silu), r=[zb], w=[sg_s])
                        V(lambda: vec.tensor_tensor(accs[0][:], accs[0][:], zz(2), ALU.mult), r=[accs[0], zb], w=[accs[0]])
                        V(lambda j=j: vec.tensor_tensor(catT[:, j, TP:TP + NS], accs[0][:], sg_s[:], ALU.mult),
                          r=[accs[0], sg_s], w=[cat_b[NB]])
                        yield
                        P(lambda: pe.transpose(GB[3][0:NS, 128:256], zc[:], ident_f[:]), r=[zc, ident_f], w=[GB[3]])
                        sr = strow[strow_i[0] % 2]; strow_i[0] += 1
                        A(lambda sr=sr: act.copy(sr[:], GB[3][0:NS, 128:256]), r=[GB[3]], w=[sr])
                        dma("sp", ca_s[l, :, 1, j * 128:(j + 1) * 128], sr[:], r=[sr])

                def head_gen(heads, B, S):
                    hs_tok, hist_s, accs, kq_s, sg_s, t_s, sqs, Sin, k_tok, u_tok, KM, strow = S
                    Sout = Sin
                    strow_i = [0]
                    for hd in heads:
                        zb = zsamp_b[8 + hd]
                        zz = lambda i: zsamp[:, (8 + hd) * 4 + i, :]
                        cw = lambda tn, tap: cwq[:, tn * 8 + hd, tap:tap + 1]
                        for tn in range(3):
                            dma("sp", hs_tok[:, tn * 3:(tn + 1) * 3, :],
                                scq[l, :, :, tn * DC + hd * 128: tn * DC + (hd + 1) * 128], w=[hs_tok])
                        dma("sp", Sin[:], ssm[l, :, hd, :, :].rearrange("r k v -> k r v"), w=[Sin])
                        for i9 in range(9):
                            yield
                            P(lambda i9=i9: pe.transpose(B[3][:, i9 * NS:(i9 + 1) * NS], hs_tok[:, i9, :],
                                                         ident_f[0:NS, 0:NS]), r=[hs_tok, ident_f], w=[B[3]])
                        A(lambda: act.copy(hist_s[:].rearrange("p a b -> p (a b)"), B[3][:, 0:9 * NS]),
                          r=[B[3]], w=[hist_s])
                        for tn in range(3):
                            V(lambda tn=tn: vec.tensor_scalar(accs[tn][:], zz(tn), cw(tn, 3), None, ALU.mult),
                              r=[zb, cwq], w=[accs[tn]])
                            for tap in (2, 1, 0):
                                V(lambda tn=tn, tap=tap: vec.scalar_tensor_tensor(
                                    accs[tn][:], hist_s[:, tn * 3 + tap, :], cw(tn, tap), accs[tn][:],
                                    ALU.mult, ALU.add), r=[hist_s, cwq, accs[tn]], w=[accs[tn]])
                            A(lambda tn=tn: act.activation(out=accs[tn][:], in_=accs[tn][:], func=AF.Silu),
                              r=[accs[tn]], w=[accs[tn]])
                            yield
                            P(lambda tn=tn: pe.transpose(B[3][0:NS, tn * 128:(tn + 1) * 128], zz(tn), ident_f[:]),
                              r=[zb, ident_f], w=[B[3]])
                            sr = strow[strow_i[0] % 2]; strow_i[0] += 1
                            A(lambda tn=tn, sr=sr: act.copy(sr[:], B[3][0:NS, tn * 128:(tn + 1) * 128]),
                              r=[B[3]], w=[sr])
                            dma("sp", cq_s[l, :, 2, tn * DC + hd * 128: tn * DC + (hd + 1) * 128], sr[:], r=[sr])
                        A(lambda: act.activation(out=sg_s[:], in_=zz(3), func=AF.Silu), r=[zb], w=[sg_s])
                        for tn, slot_i, scl in ((0, 1, HD ** -0.5), (1, 0, 1.0)):
                            V(lambda tn=tn: vec.tensor_tensor(sqs[:], accs[tn][:], accs[tn][:], ALU.mult),
                              r=[accs[tn]], w=[sqs])
                            yield
                            P(lambda: pe.matmul(B[0][:, 0:NS], ones_f[:], sqs[:], start=True, stop=True),
                              r=[ones_f, sqs], w=[B[0]])
                            A(lambda: act.activation(out=t_s[0][:], in_=B[0][:, 0:NS], func=AF.Ln, bias=EPSB[:, 0:1]),
                              r=[B[0], EPSB], w=[t_s[0]])
                            A(lambda: act.activation(out=t_s[0][:], in_=t_s[0][:], func=AF.Exp, scale=-0.5),
                              r=[t_s[0]], w=[t_s[0]])
                            V(lambda tn=tn, slot_i=slot_i, scl=scl: vec.scalar_tensor_tensor(
                                kq_s[:, :, slot_i], accs[tn][:], scl, t_s[0][:], ALU.mult, ALU.mult),
                              r=[accs[tn], t_s[0]], w=[kq_s])
                        for r_ in range(NS):
                            yield
                            P(lambda r_=r_: pe.matmul(B[0][:, 64 + r_ * 2: 64 + r_ * 2 + 2], Sin[:, r_, :],
                                                      kq_s[:, r_, :], start=True, stop=True),
                              r=[Sin, kq_s], w=[B[0]])
                        skq = lambda i: B[0][:, 64:64 + 2 * NS].rearrange("p (r c) -> p r c", c=2)[:, :, i]
                        abc, bbc = sa_bc[:, hd, :], sb_bc[:, hd, :]
                        u_ = t_s[1]
                        V(lambda: vec.tensor_tensor(u_[:], skq(0), abc, ALU.mult), r=[B[0], sa_bc], w=[u_])
                        V(lambda: vec.tensor_tensor(u_[:], accs[2][:], u_[:], ALU.subtract), r=[accs[2], u_], w=[u_])
                        V(lambda: vec.tensor_tensor(u_[:], u_[:], bbc, ALU.mult), r=[u_, sb_bc], w=[u_])
                        V(lambda: vec.tensor_tensor(sqs[:], kq_s[:, :, 0], kq_s[:, :, 1], ALU.mult), r=[kq_s], w=[sqs])
                        yield
                        P(lambda: pe.matmul(B[1][:, 0:NS], ones_f[:], sqs[:], start=True, stop=True),
                          r=[ones_f, sqs], w=[B[1]])
                        o_ = t_s[2]
                        V(lambda: vec.tensor_tensor(o_[:], skq(1), abc, ALU.mult), r=[B[0], sa_bc], w=[o_])
                        V(lambda: vec.tensor_tensor(t_s[3][:], u_[:], B[1][:, 0:NS], ALU.mult), r=[u_, B[1]], w=[t_s[3]])
                        V(lambda: vec.tensor_tensor(o_[:], o_[:], t_s[3][:], ALU.add), r=[o_, t_s[3]], w=[o_])
                        V(lambda: vec.tensor_tensor(sqs[:], o_[:], o_[:], ALU.mult), r=[o_], w=[sqs])
                        yield
                        P(lambda: pe.matmul(B[1][:, 32:32 + NS], ones_f[:], sqs[:], start=True, stop=True),
                          r=[ones_f, sqs], w=[B[1]])
                        A(lambda: act.activation(out=t_s[3][:], in_=B[1][:, 32:32 + NS], func=AF.Ln,
                                                 scale=1.0 / HD, bias=EPSB[:, 0:1]), r=[B[1], EPSB], w=[t_s[3]])
                        A(lambda: act.activation(out=t_s[3][:], in_=t_s[3][:], func=AF.Exp, scale=-0.5),
                          r=[t_s[3]], w=[t_s[3]])
                        V(lambda: vec.tensor_tensor(o_[:], o_[:], t_s[3][:], ALU.mult), r=[o_, t_s[3]], w=[o_])
                        V(lambda: vec.scalar_tensor_tensor(catT[:, 8 + hd, TP:TP + NS], o_[:], ong[:, 0:1], sg_s[:],
                                                           ALU.mult, ALU.mult), r=[o_, ong, sg_s], w=[cat_b[NB]])
                        yield
                        P(lambda: pe.transpose(B[1][0:NS, 128:256], kq_s[:, :, 0], ident_f[:]),
                          r=[kq_s, ident_f], w=[B[1]])
                        yield
                        P(lambda: pe.transpose(B[1][0:NS, 256:384], u_[:], ident_f[:]), r=[u_, ident_f], w=[B[1]])
                        A(lambda: act.copy(k_tok[:], B[1][0:NS, 128:256]), r=[B[1]], w=[k_tok])
                        A(lambda: act.copy(u_tok[:], B[1][0:NS, 256:384]), r=[B[1]], w=[u_tok])
                        for r_ in range(NS):
                            if r_ % 8 == 0:
                                V(lambda r_=r_: vec.tensor_tensor(
                                    KM[:], bc3(k_tok[:, None, :], [NS, 8, 128]),
                                    bc3(ident_f[0:NS, r_:r_ + 8, None], [NS, 8, 128]), ALU.mult),
                                  r=[k_tok, ident_f], w=[KM])
                            bk = B[2 + (r_ // 4) % 2]
                            yield
                            P(lambda r_=r_, bk=bk: pe.matmul(bk[:, (r_ % 4) * 128:(r_ % 4 + 1) * 128], KM[:, r_ % 8, :],
                                                             u_tok[:], start=True, stop=True),
                              r=[KM, u_tok], w=[bk])
                            V(lambda r_=r_, bk=bk: vec.scalar_tensor_tensor(
                                Sout[:, r_, :], Sin[:, r_, :], sa_bc[:, hd, r_:r_ + 1],
                                bk[:, (r_ % 4) * 128:(r_ % 4 + 1) * 128], ALU.mult, ALU.add),
                              r=[Sin, sa_bc, bk], w=[Sout])
                        dma("sp", ss_s[l, :, hd, :, :].rearrange("r k v -> k r v"), Sout[:], r=[Sout])


                for _ in conv_gen():
                    pass
                gens = [head_gen([0, 2, 4, 6], GB, head_scratch()), head_gen([1, 3, 5, 7], IP, head_scratch())]
                while gens:
                    for g_ in list(gens):
                        try:
                            next(g_)
                        except StopIteration:
                            gens.remove(g_)
            if STOP in ('gdn', 'gdn_p', 'gdn_s'):
                mk.finish(); return nc
            if ps_ == NPASS - 1:
                tailo = carve([128, 128])
                P(lambda: pe.transpose(GB[0][0:16, 0:128], tailA[:].rearrange("p r j -> p (r j)"), ident_f[:]),
                  r=[tailA, ident_f], w=[GB[0]])
                A(lambda: act.copy(tailo[0:16, :], GB[0][0:16, 0:128]), r=[GB[0]], w=[tailo])
                dma("sp", ca_p[l, :, :].rearrange("r (j p) -> (r j) p", p=128), tailo[0:16, :], r=[tailo])
                P(lambda: pe.transpose(GB[0][0:72, 128:256], tailQ[:].rearrange("p r j -> p (r j)"), ident_f[:]),
                  r=[tailQ, ident_f], w=[GB[0]])
                tailo2 = carve([128, 128])
                A(lambda: act.copy(tailo2[0:72, :], GB[0][0:72, 128:256]), r=[GB[0]], w=[tailo2])
                dma("sp", cq_p[l, :, :].rearrange("r (j p) -> (r j) p", p=128), tailo2[0:72, :], r=[tailo2])

            barrier()
            NR = 8
            xc = [carve([128, 256]) for _ in range(NR)]
            gp = [carve([128, 256]) for _ in range(2)]
            gsm = [carve([NS, 256]) for _ in range(2)]
            yt = [carve([128, 256]) for _ in range(NR)]
            c_has_s = (ps_ == NPASS - 1) if NPASS >= 2 else do_s
            ntile_c = NTT + (1 if c_has_s else 0)
            for u in range(8):
                slot = wslot()
                gpi, gsi = gp[u % 2], gsm[u % 2]
                dma("sp", gpi[:], gscr[l, 0:1, u * 256:(u + 1) * 256].to_broadcast([128, 256]), r=[GSCR], w=[gpi])
                if c_has_s:
                    dma("sp", gsi[:], gscr[l, 1:17, u * 256:(u + 1) * 256], r=[GSCR], w=[gsi])
                for tt in range(ntile_c):
                    is_s = (tt == NTT)
                    np_ = NS if is_s else 128
                    c0 = TP if is_s else tt * 128
                    cb = cat_b[NB] if is_s else cat_b[tt // 4]
                    k_ = (u * ntile_c + tt)
                    bank = IP[k_ % 4]
                    xci, yti = xc[k_ % NR], yt[k_ % NR]
                    if is_s:
                        gt_ = T // 128
                        sap = (xs[:, u * 256:(u + 1) * 256] if l == 0 else xscr[T:T + NS, u * 256:(u + 1) * 256])
                        dap = xscr[T:T + NS, u * 256:(u + 1) * 256]
                    else:
                        r0 = t0 + tt * 128
                        gt_ = r0 // 128
                        sap = (xp if l == 0 else xscr)[r0:r0 + 128, u * 256:(u + 1) * 256]
                        dap = xscr[r0:r0 + 128, u * 256:(u + 1) * 256]
                    dstT = xb(u, gt_)
                    dma("sp", xci[0:np_, :], sap, r=([] if l == 0 else [dstT]), w=[xci])
                    for kc in range(KC):
                        P(lambda kc=kc, bank=bank: pe.matmul(bank[0:np_, 0:256], catT[:, kc, c0:c0 + np_],
                                                             slot[:, kc * 256:(kc + 1) * 256],
                                                             start=(kc == 0), stop=(kc == KC - 1)),
                          r=[cb, slot], w=[bank])
                    gt = gsi if is_s else gpi
                    V(lambda bank=bank, gt=gt, yti=yti: vec.tensor_tensor(yti[0:np_, :], bank[0:np_, 0:256],
                                                                          gt[0:np_, :], ALU.mult),
                      r=[bank, gt], w=[yti])
                    V(lambda xci=xci, yti=yti: vec.tensor_tensor(yti[0:np_, :], yti[0:np_, :], xci[0:np_, :], ALU.add),
                      r=[yti, xci], w=[yti])
                    dma("act", dap, yti[0:np_, :], r=[yti], w=[dstT])
                wdone()

    if STOP == 'C':
        mk.finish(); return nc
    barrier()
    xt = [carve([128, D]) for _ in range(2)]
    xn = [carve([128, D], BF16) for _ in range(2)]
    stat = [carve([128, 4]) for _ in range(2)]
    fng = carve([128, D])
    dma("sp", fng[:], fng_d[:, :], w=[fng])
    for tt in range(T // 128 + 1):
        is_s = (tt == T // 128)
        np_ = NS if is_s else 128
        xti, sti = xt[tt % 2], stat[tt % 2]
        junk = xn[tt % 2]
        rb = [xb(u_, tt) for u_ in range(8)]
        if is_s:
            dma("sp", xti[0:np_, :], xscr[T:T + NS, :], r=rb, w=[xti])
        else:
            dma("sp", xti[0:np_, :], xscr[tt * 128:(tt + 1) * 128, :], r=rb, w=[xti])
        A(lambda: act.activation(out=junk[0:np_, :], in_=xti[0:np_, :], func=AF.Square, accum_out=sti[0:np_, 0:1]),
          r=[xti], w=[junk, sti])
        V(lambda: vec.tensor_scalar(sti[0:np_, 1:2], sti[0:np_, 0:1], 1.0 / D, EPS, ALU.mult, ALU.add), r=[sti], w=[sti])
        A(lambda: act.activation(out=sti[0:np_, 2:3], in_=sti[0:np_, 1:2], func=AF.Ln), r=[sti], w=[sti])
        A(lambda: act.activation(out=sti[0:np_, 3:4], in_=sti[0:np_, 2:3], func=AF.Exp, scale=-0.5), r=[sti], w=[sti])
        V(lambda: vec.scalar_tensor_tensor(xti[0:np_, :], xti[0:np_, :], sti[0:np_, 3:4], fng[0:np_, :],
                                           ALU.mult, ALU.mult), r=[xti, sti, fng], w=[xti])
        if is_s:
            dma("sp", y_s[:, :], xti[0:np_, :], r=[xti])
        else:
            dma("sp", y_p[tt * 128:(tt + 1) * 128, :], xti[0:np_, :], r=[xti])
    mk.finish()
    return nc


def _prep_shared(inp, L):
    f = np.float32
    w_in, w_ada, w_out = inp["w_in"], inp["w_ada"], inp["w_out"]
    wall = np.empty((L, NUNIT, 128, UNIT), f)

    def coltile(w, c0):
        return w[:, c0:c0 + 128].reshape(KC, 128, 128).transpose(1, 0, 2)

    for l in range(L):
        for u in range(24):
            for ct in range(2):
                wall[l, u, :, ct * 2048:(ct + 1) * 2048] = coltile(w_ada[l], (u * 2 + ct) * 128).reshape(128, 2048)
        u = 24
        for j in range(8):
            for pair in ((1, 2), (0, 3)):
                for ct, grp in enumerate(pair):
                    wall[l, u, :, ct * 2048:(ct + 1) * 2048] = coltile(w_in[l], grp * DC + j * 128).reshape(128, 2048)
                u += 1
        for hd in range(8):
            cols = [4 * DC + hd * 128, 5 * DC + hd * 128, 6 * DC + hd * 128, 7 * DC + hd * 128]
            for pair in ((0, 1), (2, 3)):
                for ct, ci in enumerate(pair):
                    wall[l, u, :, ct * 2048:(ct + 1) * 2048] = coltile(w_in[l], cols[ci]).reshape(128, 2048)
                u += 1
        for g in range(8):
            wall[l, u] = w_out[l][:, g * 256:(g + 1) * 256].reshape(KC, 128, 256).transpose(1, 0, 2).reshape(128, UNIT)
            u += 1
    sh = {"wall": wall}
    sh["wba"] = np.ascontiguousarray(
        w_in[:L, :, 8 * DC:8 * DC + 16].reshape(L, KC, 128, 16).transpose(0, 2, 1, 3).reshape(L, 128, KC * 16))
    sh["ng"] = np.ascontiguousarray(inp["norm_g"][:L].reshape(L, KC, 128).transpose(0, 2, 1))
    sh["bada"] = np.ascontiguousarray(inp["b_ada"][:L].reshape(L, 48, 128).transpose(0, 2, 1))
    sh["cwa"] = np.ascontiguousarray(inp["conv_a_w"][:L].reshape(L, 3, 8, 128).transpose(0, 3, 2, 1).reshape(L, 128, 24))
    sh["cwq"] = np.ascontiguousarray(inp["conv_qkv_w"][:L].reshape(L, 4, 24, 128).transpose(0, 3, 2, 1).reshape(L, 128, 96))
    sh["alog"] = np.ascontiguousarray(np.broadcast_to(inp["a_log"][:L, None, :], (L, 64, 8)))
    sh["dtb"] = np.ascontiguousarray(np.broadcast_to(inp["dt_bias"][:L, None, :], (L, 64, 8)))
    sh["ong"] = np.ascontiguousarray(inp["o_norm_g"][:L].reshape(L, 128, 1))
    sh["fng"] = np.ascontiguousarray(np.broadcast_to(inp["final_norm_g"][None, :], (128, D)))
    return sh


_CACHE = {}


def run(inp, T, L, TP, n_cores=8):
    key = (T, L, TP)
    if key not in _CACHE:
        _CACHE[key] = build_program(T, L, TP)
    nc = _CACHE[key]
    f = np.float32
    sh = _prep_shared({k: np.asarray(v, f) for k, v in inp.items()}, L)
    B = inp["x_prompt"].shape[0]
    in_maps = []
    for c in range(n_cores):
        b = c % B
        rows = slice(c * NS, (c + 1) * NS)
        cc = np.concatenate([inp["c_prompt"][b:b + 1], inp["c_sample"][rows]], axis=0)
        m = dict(sh)
        m["xp"] = np.ascontiguousarray(inp["x_prompt"][b, :T], f)
        m["xs"] = np.ascontiguousarray(inp["x_sample"][rows, 0, :], f)
        m["cT"] = np.ascontiguousarray(cc.reshape(17, KC, 128).transpose(2, 1, 0), f)
        m["sca"] = np.ascontiguousarray(inp["state_conv_a"][:L, rows], f)
        m["scq"] = np.ascontiguousarray(inp["state_conv_qkv"][:L, rows], f)
        m["ssm"] = np.ascontiguousarray(inp["state_ssm"][:L, rows], f)
        in_maps.append(m)
    res = run_bass_kernel_spmd(nc, in_maps, core_ids=list(range(n_cores)))
    R = res.results
    nb = min(B, n_cores)
    y_p = np.stack([R[b]["y_p"] for b in range(nb)])
    ca_p = np.stack([R[b]["ca_p"] for b in range(nb)], axis=1)
    cq_p = np.stack([R[b]["cq_p"] for b in range(nb)], axis=1)
    ss_p = np.stack([R[b]["ss_p"] for b in range(nb)], axis=1)
    y_s = np.concatenate([R[c]["y_s"] for c in range(n_cores)], axis=0)[:, None, :]
    ca_s = np.concatenate([R[c]["ca_s"] for c in range(n_cores)], axis=1)
    cq_s = np.concatenate([R[c]["cq_s"] for c in range(n_cores)], axis=1)
    ss_s = np.concatenate([R[c]["ss_s"] for c in range(n_cores)], axis=1)
    return tuple(np.ascontiguousarray(a, dtype=f) for a in (y_p, y_s, ca_p, cq_p, ss_p, ca_s, cq_s, ss_s))


def kernel(**inputs):
    inp = {k: np.asarray(v) for k, v in inputs.items()}
    return run(inp, T=2048, L=4, TP=1024, n_cores=8)
```
